# Optimizing a Trainium2 kernel written in Bass

```python
import math
import jax, jax.numpy as jnp
from jax import lax
import numpy as np

D_MODEL = 2048
BATCH = 2
SEQ = 8192
DEPTH = 4

MIX_WIDTH = D_MODEL
A_V = 128
A_HEADS = MIX_WIDTH // 2 // A_V
A_NOPE = 128
A_ROPE = 64
A_Q_LORA = 512
A_KV_LORA = 256
A_WIDTH = A_HEADS * A_V
B_HEAD_DIM = 128
B_HEADS = (MIX_WIDTH - A_WIDTH) // B_HEAD_DIM
B_WIDTH = B_HEADS * B_HEAD_DIM
DILATED_PATTERNS = ((128, 1), (512, 4), (2048, 16))
NUM_BUCKETS = 32
MAX_DISTANCE = 1024
ROPE_THETA = 10000.0
Q_BLOCK = 128
EPS = 1e-6
IN_SPLITS = (A_Q_LORA, A_KV_LORA, A_ROPE, A_WIDTH, B_WIDTH, B_WIDTH, B_WIDTH, B_WIDTH)
IN_WIDTH = sum(IN_SPLITS)

kernel_name = "hybrid_mla_dilated_adaln_encoder"


def rms_norm(x, g):
    xf = x.astype(jnp.float32)
    y = xf * lax.rsqrt(jnp.mean(xf * xf, axis=-1, keepdims=True) + EPS)
    return (y * g.astype(jnp.float32)).astype(x.dtype)


def rope_angles(positions):
    inv = 1.0 / (ROPE_THETA ** (jnp.arange(0, A_ROPE, 2, dtype=jnp.float32) / A_ROPE))
    ang = positions.astype(jnp.float32)[..., None] * inv
    return jnp.cos(ang), jnp.sin(ang)


def apply_rope(t, cos, sin):
    tf = t.astype(jnp.float32)
    t1, t2 = jnp.split(tf, 2, axis=-1)
    return jnp.concatenate([t1 * cos - t2 * sin, t2 * cos + t1 * sin], axis=-1).astype(t.dtype)


def mla_attention(qn, qr, kn, kr, v):
    bsz, s_len, h, _ = qn.shape
    nblk = s_len // Q_BLOCK
    scale = 1.0 / math.sqrt(A_NOPE + A_ROPE)

    def to_blocks(t):
        return t.reshape(bsz, nblk, Q_BLOCK, *t.shape[2:]).swapaxes(0, 1)

    def one_block(args):
        qn_b, qr_b = args
        s = (jnp.einsum('bqhd,bkhd->bhqk', qn_b, kn)
             + jnp.einsum('bqhr,bkr->bhqk', qr_b, kr)).astype(jnp.float32) * scale
        p = jax.nn.softmax(s, axis=-1).astype(v.dtype)
        return jnp.einsum('bhqk,bkhd->bqhd', p, v)

    o = lax.map(one_block, (to_blocks(qn), to_blocks(qr)))
    return o.swapaxes(0, 1).reshape(bsz, s_len, h * v.shape[-1])


def dilated_offsets(window, dilation):
    half = window // 2
    return np.arange(-half, half + 1, dilation, dtype=np.int32)


def t5_buckets(rel):
    nb = NUM_BUCKETS // 2
    max_exact = nb // 2
    base = np.where(rel > 0, nb, 0)
    n = np.abs(rel)
    large = max_exact + (np.log(np.maximum(n, 1) / max_exact)
                         / math.log(MAX_DISTANCE / max_exact) * (nb - max_exact)).astype(np.int32)
    large = np.minimum(large, nb - 1)
    return (base + np.where(n < max_exact, n, large)).astype(np.int32)


def dilated_attention(q, k, v, rel_bias):
    bsz, s_len, h, dh = q.shape
    nblk = s_len // Q_BLOCK
    scale = dh ** -0.5
    patterns = []
    for window, dilation in DILATED_PATTERNS:
        off = dilated_offsets(window, dilation)
        bias = rel_bias[t5_buckets(off)].T.astype(jnp.float32)
        patterns.append((jnp.asarray(off), bias))
    q_blocks = q.reshape(bsz, nblk, Q_BLOCK, h, dh).swapaxes(0, 1)
    starts = jnp.arange(nblk, dtype=jnp.int32) * Q_BLOCK

    def one_block(args):
        q_b, start = args
        qpos = start + jnp.arange(Q_BLOCK, dtype=jnp.int32)
        maxes, denoms, outs = [], [], []
        for off, bias in patterns:
            idx = qpos[:, None] + off[None, :]
            valid = (idx >= 0) & (idx < s_len)
            idx = jnp.clip(idx, 0, s_len - 1)
            k_g = jnp.take(k, idx, axis=1)
            v_g = jnp.take(v, idx, axis=1)
            s = jnp.einsum('bqhd,bqjhd->bhqj', q_b, k_g).astype(jnp.float32) * scale
            s = jnp.where(valid[None, None], s + bias[None, :, None, :], -jnp.inf)
            m = jnp.max(s, axis=-1, keepdims=True)
            p = jnp.exp(s - m)
            den = jnp.sum(p, axis=-1, keepdims=True)
            o = jnp.einsum('bhqj,bqjhd->bhqd', (p / den).astype(v.dtype), v_g).astype(jnp.float32)
            maxes.append(m)
            denoms.append(den)
            outs.append(o)
        m_all = jnp.stack(maxes)
        wts = jnp.stack(denoms) * jnp.exp(m_all - jnp.max(m_all, axis=0, keepdims=True))
        out = jnp.sum(wts * jnp.stack(outs), axis=0) / jnp.sum(wts, axis=0)
        return out.transpose(0, 2, 1, 3).astype(q.dtype)

    o = lax.map(one_block, (q_blocks, starts))
    return o.swapaxes(0, 1).reshape(bsz, s_len, h * dh)


def setup_inputs(seed: int = 0) -> dict:
    key = jax.random.key(seed)
    ks = jax.random.split(key, 16)
    f32 = jnp.float32

    def nrm(k, shape, fan_in):
        return jax.random.normal(k, shape, f32) * fan_in ** -0.5

    x = jax.random.normal(ks[0], (BATCH, SEQ, D_MODEL), f32)
    c = jax.random.normal(ks[1], (BATCH, D_MODEL), f32)
    positions = (jnp.arange(SEQ, dtype=jnp.int32)[None, :]
                 + jax.random.randint(ks[2], (BATCH, 1), 0, 1024, dtype=jnp.int32))
    norm_g = 1.0 + 0.02 * jax.random.normal(ks[3], (DEPTH, D_MODEL), f32)
    ada_w = 0.5 * nrm(ks[4], (DEPTH, D_MODEL, 3 * D_MODEL), D_MODEL)
    ada_b = 0.02 * jax.random.normal(ks[5], (DEPTH, 3 * D_MODEL), f32)
    w_in = nrm(ks[6], (DEPTH, D_MODEL, IN_WIDTH), D_MODEL)
    q_a_norm_g = 1.0 + 0.02 * jax.random.normal(ks[7], (DEPTH, A_Q_LORA), f32)
    w_q_up = nrm(ks[8], (DEPTH, A_Q_LORA, A_HEADS * (A_NOPE + A_ROPE)), A_Q_LORA)
    kv_a_norm_g = 1.0 + 0.02 * jax.random.normal(ks[9], (DEPTH, A_KV_LORA), f32)
    w_kv_up = nrm(ks[10], (DEPTH, A_KV_LORA, A_HEADS * (A_NOPE + A_V)), A_KV_LORA)
    rel_bias = 0.5 * jax.random.normal(ks[11], (NUM_BUCKETS, B_HEADS), f32)
    w_out = nrm(ks[12], (DEPTH, MIX_WIDTH, D_MODEL), MIX_WIDTH)
    final_norm_g = 1.0 + 0.02 * jax.random.normal(ks[13], (D_MODEL,), f32)
    return {"x": x, "c": c, "positions": positions, "norm_g": norm_g,
            "ada_w": ada_w, "ada_b": ada_b, "w_in": w_in,
            "q_a_norm_g": q_a_norm_g, "w_q_up": w_q_up,
            "kv_a_norm_g": kv_a_norm_g, "w_kv_up": w_kv_up,
            "rel_bias": rel_bias, "w_out": w_out, "final_norm_g": final_norm_g}


def reference(x, c, positions, norm_g, ada_w, ada_b, w_in, q_a_norm_g, w_q_up,
              kv_a_norm_g, w_kv_up, rel_bias, w_out, final_norm_g):
    bsz, s_len, _ = x.shape
    cos, sin = rope_angles(positions)
    c_act = jax.nn.silu(c)
    split_points = np.cumsum(IN_SPLITS)[:-1].tolist()
    for l in range(DEPTH):
        shift, scale, gate = jnp.split(c_act @ ada_w[l] + ada_b[l], 3, axis=-1)
        h = rms_norm(x, norm_g[l]) * (1.0 + scale[:, None, :]) + shift[:, None, :]
        cq, ckv, k_rope, gate_a, q_b, k_b, v_b, gate_b = jnp.split(h @ w_in[l], split_points, axis=-1)
        q_a = (rms_norm(cq, q_a_norm_g[l]) @ w_q_up[l]).reshape(bsz, s_len, A_HEADS, A_NOPE + A_ROPE)
        q_nope = q_a[..., :A_NOPE]
        q_rope = apply_rope(q_a[..., A_NOPE:], cos[:, :, None, :], sin[:, :, None, :])
        kv_a = (rms_norm(ckv, kv_a_norm_g[l]) @ w_kv_up[l]).reshape(bsz, s_len, A_HEADS, A_NOPE + A_V)
        k_nope, v_a = kv_a[..., :A_NOPE], kv_a[..., A_NOPE:]
        k_rope = apply_rope(k_rope, cos, sin)
        y_a = mla_attention(q_nope, q_rope, k_nope, k_rope, v_a) * jax.nn.silu(gate_a)
        y_b = dilated_attention(q_b.reshape(bsz, s_len, B_HEADS, B_HEAD_DIM),
                                k_b.reshape(bsz, s_len, B_HEADS, B_HEAD_DIM),
                                v_b.reshape(bsz, s_len, B_HEADS, B_HEAD_DIM),
                                rel_bias) * jax.nn.silu(gate_b)
        y = jnp.concatenate([y_a, y_b], axis=-1) @ w_out[l]
        x = x + gate[:, None, :] * y
    return rms_norm(x, final_norm_g)
```

```python
import contextlib
import math

import numpy as np
import concourse.bass as bass
import concourse.mybir as mybir
from concourse.bass_utils import run_bass_kernel_spmd

F32 = mybir.dt.float32
BF16 = mybir.dt.bfloat16
I32 = mybir.dt.int32
AF = mybir.ActivationFunctionType
ALU = mybir.AluOpType

ENGS = ("pe", "act", "dve", "pool", "sp")
_NOCC = False
_PEWAIT = False

D = 2048
T = 2048
S = 8192
DEPTH = 4
NH = 8
INW = 5952
EPS = 1e-6
NBLK = 10
XROWS = 256 * NBLK
KB_ROW = 384
VB_ROW = 1536
TWW = 2944
SC_A = 1.0 / math.sqrt(192.0)
SC_B = 1.0 / math.sqrt(128.0)
C_IN = dict(cq=0, ckv=512, kr=768, ga=832, qb=1856, kb=2880, vb=3904, gb=4928)


class Op:
    __slots__ = ("eng", "fn", "deps", "marked", "dma_key", "tok", "ndma", "inc", "nowait")

    def __init__(self, eng, fn, dma_key, ndma, inc=16):
        self.inc = inc
        self.nowait = False
        self.eng = eng
        self.fn = fn
        self.deps = ()
        self.marked = False
        self.dma_key = dma_key
        self.tok = 0
        self.ndma = ndma


class Sched:
    def __init__(self, nc):
        self.nc = nc
        self.ops = {e: [] for e in ENGS}
        self.bufs = {}
        self.dkeys = {}
        self.final = []
        self.last_dma = {}
        self.bar = None
        self.bar_passed = set()

    def barrier(self):
        deps = set()
        for e in ENGS:
            if self.ops[e]:
                deps.add(self.ops[e][-1])
        deps.update(self.last_dma.values())
        self.bar = deps
        self.bar_passed = set()

    def op(self, eng, fn, reads=(), writes=(), dma_key=None, ndma=1, inc=16, nowait=False):
        o = Op(eng, fn, dma_key, ndma, inc)
        o.nowait = nowait
        deps = set()
        bufs = self.bufs
        for b in reads:
            st = bufs.get(b)
            if st is None:
                st = bufs[b] = [None, []]
            if st[0] is not None:
                deps.add(st[0])
        for b in writes:
            st = bufs.get(b)
            if st is None:
                st = bufs[b] = [None, []]
            if st[0] is not None:
                deps.add(st[0])
            deps.update(st[1])
        for b in reads:
            bufs[b][1].append(o)
        for b in writes:
            st = bufs[b]
            st[0] = o
            st[1] = []
        if self.bar is not None and eng not in self.bar_passed:
            deps.update(self.bar)
            self.bar_passed.add(eng)
        for d in deps:
            if not (nowait and d.eng == "pe" and eng == "pe" and d.dma_key is None):
                d.marked = True
        o.deps = deps
        if dma_key is not None:
            k = self.dkeys.get(dma_key)
            if k is None:
                k = self.dkeys[dma_key] = [None, 0]
            k[1] += ndma * inc
            o.tok = k[1]
            self.last_dma[dma_key] = o
        self.ops[eng].append(o)
        return o

    def emit(self):
        nc = self.nc
        with contextlib.ExitStack() as es:
            esem = {e: es.enter_context(nc.semaphore("e_" + e)) for e in ENGS}
            for i, (k, v) in enumerate(self.dkeys.items()):
                v[0] = es.enter_context(nc.semaphore("d%d" % i))
            for d in self.final:
                d.marked = True
            for e in ENGS:
                n = 0
                for o in self.ops[e]:
                    if o.dma_key is None and o.marked:
                        n += 1
                        o.tok = n
            block = es.enter_context(nc.Block())
            dkeys = self.dkeys
            final = self.final

            def run(e, h):
                waited = {}
                for o in self.ops[e]:
                    need = {}
                    for d in o.deps:
                        if o.nowait and d.dma_key is None and d.eng == "pe" and e == "pe":
                            continue
                        sem = dkeys[d.dma_key][0] if d.dma_key is not None else esem[d.eng]
                        if waited.get(sem, 0) < d.tok and need.get(sem, 0) < d.tok:
                            need[sem] = d.tok
                    for sem, val in need.items():
                        h.wait_ge(sem, val)
                        waited[sem] = val
                    r = o.fn(h)
                    if o.dma_key is not None:
                        sem = dkeys[o.dma_key][0]
                        if not isinstance(r, (list, tuple)):
                            r = [r]
                        assert len(r) == o.ndma, (len(r), o.ndma)
                        for ins in r:
                            ins.then_inc(sem, o.inc)
                    elif o.marked:
                        r.then_inc(esem[e], 1)
                if e == "sp":
                    for d in final:
                        sem = dkeys[d.dma_key][0] if d.dma_key is not None else esem[d.eng]
                        h.wait_ge(sem, d.tok)

            @block.tensor
            def _(h):
                run("pe", h)

            @block.scalar
            def _(h):
                run("act", h)

            @block.vector
            def _(h):
                run("dve", h)

            @block.gpsimd
            def _(h):
                run("pool", h)

            @block.sync
            def _(h):
                run("sp", h)


def build_program(depth=DEPTH, stop=99):
    nc = bass.Bass("TRN2", target_bir_lowering=False)
    SC = Sched(nc)

    def din(name, shape, dt=F32):
        return nc.dram_tensor(name, shape, dt, kind="ExternalInput").ap()

    def dint(name, shape, dt=BF16):
        return nc.dram_tensor(name, shape, dt, kind="Internal").ap()

    xT = din("xT", [D, T])
    cT = din("cT", [128, 16])
    pos = din("pos", [1, T], I32)
    normg = din("normg", [128, DEPTH, 16])
    adab = din("adab", [128, DEPTH, 48])
    qg = din("qg", [128, DEPTH, 4])
    kvg = din("kvg", [128, DEPTH, 2])
    fng = din("fng", [128, 16])
    flags = din("flags", [128, 10])
    rconst = din("rconst", [64, 2])
    biasTw = din("biasTw", [NH, 128, TWW])
    logm = din("logm", [128, TWW])
    ident = din("ident", [128, 128])
    ada_w = din("ada_wp", [DEPTH, D, 1536])
    w_in = din("w_in", [DEPTH, D, INW])
    w_q_up = din("w_q_up", [DEPTH, 512, 1536])
    w_kv_up = din("w_kv_up", [DEPTH, 256, 2048])
    w_out = din("w_out", [DEPTH, D, D])
    outT = nc.dram_tensor("outT", [D, T], F32, kind="ExternalOutput").ap()

    xres = dint("xres", [D, T], F32)
    QA = dint("QA", [NH, 192, T])
    GA = dint("GA", [NH, 128, T])
    QB = dint("QB", [NH, 128, T])
    GB = dint("GB", [NH, 128, T])
    XCH = dint("XCH", [XROWS, T])
    G = dint("G", [NBLK * 4 * 256, T])
    TW = dint("TW", [NH, 128, TWW])
    ADX = dint("ADX", [128, DEPTH * 12], F32)
    ADG = dint("ADG", [4 * 128, DEPTH * 12], F32)
    RSd = dint("RSd", [128, T], F32)
    COSd = dint("COSd", [64, T], F32)
    SGNd = dint("SGNd", [64, T], F32)
    G3 = G.rearrange("(a f) t -> a f t", f=256)
    VBv = XCH[VB_ROW:VB_ROW + 1024, :].rearrange("r (two c) -> (r two) c", two=2)

    ARENA = 206 * 1024
    arena = nc.alloc_sbuf_tensor("arena", [128, ARENA], mybir.dt.uint8)
    base = nc.lookup_mloc(arena).addr
    cur = [0]
    names = [0]

    def sbt(shape, dt):
        nbytes = int(np.prod(shape[1:])) * (2 if dt == BF16 else 4)
        nbytes = (nbytes + 63) // 64 * 64
        at = cur[0]
        cur[0] += nbytes
        assert cur[0] <= ARENA, ("SBUF overflow", cur[0])
        names[0] += 1
        return nc.alloc_sbuf_tensor_at("t%d" % names[0], shape, dt, offset=base + at)

    HY = sbt([128, 16, T], BF16)
    MOD = sbt([128, DEPTH, 48], F32)
    AMUL = sbt([128, DEPTH, 16], F32)
    NORMG = sbt([128, DEPTH, 16], F32)
    QG = sbt([128, DEPTH, 4], F32)
    KVG = sbt([128, DEPTH, 2], F32)
    FNG = sbt([128, 16], F32)
    FLG = sbt([128, 10], F32)
    EPSC = sbt([128, 1], F32)
    ONESF = sbt([128, 128], F32)
    ONESQ = sbt([128, 128], F32)
    ONESK = sbt([128, 128], F32)
    ONESB = sbt([128, 128], BF16)
    ONESL = sbt([128, 128], BF16)
    ONESR = sbt([128, 128], BF16)
    CT = sbt([128, 16], F32)
    CACT = sbt([128, 16], BF16)
    ADAB = sbt([128, DEPTH, 48], F32)
    RC = sbt([64, 2], F32)
    IDB = sbt([128, 128], BF16)
    ONES1 = sbt([128, 128], F32)
    ONESFb = sbt([128, 128], BF16)
    phase_base = cur[0]

    def phase():
        SC.barrier()
        cur[0] = phase_base

    PSB = [nc.alloc_psum_tensor("ps%d" % i, [128, 512], F32) for i in range(8)]

    cnt = {"ps": 0, "os": 0, "ws": 0, "pt": 0, "ev": 0}

    def next_ps(n=4):
        i = cnt["ps"] % n
        cnt["ps"] += 1
        return i

    def next_os():
        i = cnt["os"] % 4
        cnt["os"] += 1
        return i

    def next_ws():
        i = cnt["ws"] % 3
        cnt["ws"] += 1
        return i

    def dma(q, out, in_, reads, writes, key):
        return SC.op(q, lambda h, o=out, i=in_: h.dma_start(out=o, in_=i), reads=reads, writes=writes, dma_key=key)

    def dmas(q, pairs, reads, writes, key):
        return SC.op(q, lambda h, pairs=pairs: [h.dma_start(out=o, in_=i) for (o, i) in pairs], reads=reads, writes=writes,
                     dma_key=key, ndma=len(pairs))

    def mm_group(out, pairs, reads, writes, nowait=False):
        n = len(pairs)

        def fn(h, out=out, pairs=pairs, n=n):
            r = None
            for i, (l, rr) in enumerate(pairs):
                r = h.matmul(out, lhsT=l, rhs=rr, start=(i == 0), stop=(i == n - 1))
            return r
        return SC.op("pe", fn, reads=reads, writes=writes, nowait=nowait)

    def mm1(out, lhsT, rhs, start, stop, reads, writes, nowait=False):
        return SC.op("pe", lambda h, o=out, l=lhsT, r=rhs, a=start, b=stop: h.matmul(o, lhsT=l, rhs=r, start=a, stop=b),
                     reads=reads, writes=writes, nowait=nowait and not _PEWAIT)

    def act(out, in_, func, reads, writes, scale=None, bias=None):
        kw = {}
        if scale is not None:
            kw["scale"] = scale
        if bias is not None:
            kw["bias"] = bias
        return SC.op("act", lambda h, o=out, i=in_, f=func, kw=kw: h.activation(out=o, in_=i, func=f, **kw), reads=reads, writes=writes)

    def tt(out, in0, in1, op, reads, writes, eng="dve"):
        return SC.op(eng, lambda h, o=out, a=in0, b=in1, op=op: h.tensor_tensor(out=o, in0=a, in1=b, op=op), reads=reads, writes=writes)

    def ts(out, in0, s1, s2, op0, op1, reads, writes, eng="dve"):
        return SC.op(eng, lambda h, o=out, a=in0, s1=s1, s2=s2, op0=op0, op1=op1: h.tensor_scalar(
            out=o, in0=a, scalar1=s1, scalar2=s2, op0=op0, op1=op1), reads=reads, writes=writes)

    def stt(out, in0, scalar, in1, op0, op1, reads, writes, eng="dve"):
        return SC.op(eng, lambda h, o=out, a=in0, sc=scalar, b=in1, op0=op0, op1=op1: h.scalar_tensor_tensor(
            out=o, in0=a, scalar=sc, in1=b, op0=op0, op1=op1), reads=reads, writes=writes)

    def recip(out, in_, reads, writes):
        return SC.op("dve", lambda h, o=out, i=in_: h.reciprocal(out=o, in_=i), reads=reads, writes=writes)

    def cp(eng, out, in_, reads, writes):
        if eng == "act":
            return act(out, in_, AF.Copy, reads, writes)
        return SC.op(eng, lambda h, o=out, i=in_: h.tensor_copy(out=o, in_=i), reads=reads, writes=writes)

    def memset(t, val, name):
        return SC.op("dve", lambda h, t=t, v=val: h.memset(t, v), writes=[name])

    def evac_eng():
        cnt["ev"] += 1
        return "act" if cnt["ev"] % 2 else "dve"

    rank_cache = {}

    def rank(h):
        return h.partition_id() % 4

    def xt(c, tb):
        return ("x", c, tb)

    def hy_tb(tb):
        return [("HY", c, tb) for c in range(16)]

    phase()
    WS0 = [sbt([128, 16, 256], BF16) for _ in range(3)]
    for (dst, src, nm) in ((NORMG, normg, "NORMG"), (QG, qg, "QG"), (KVG, kvg, "KVG"), (FNG, fng, "FNG"),
                           (FLG, flags, "FLG"), (CT, cT, "CT"), (ADAB, adab, "ADAB"), (RC, rconst, "RC")):
        dma("sp", dst[:], src, [], [nm], nm)
    dma("pool", IDB[:, :], ident, [], ["IDB"], "IDB")
    memset(EPSC[:, :], EPS, "EPSC")
    memset(ONESF[:, :], 1.0 / 2048, "ONESF")
    memset(ONESQ[:, :], 1.0 / 512, "ONESQ")
    memset(ONESK[:, :], 1.0 / 256, "ONESK")
    memset(ONESB[:, :], 1.0, "ONESB")
    memset(ONES1[:, :], 1.0, "ONES1")
    memset(ONESFb[:, :], 1.0 / 2048, "ONESFb")
    ts(ONESL[:, :], ONESB[:, :], FLG[:, 0:1], 0.0, ALU.mult, ALU.add, ["ONESB", "FLG"], ["ONESL"])
    ts(ONESR[:, :], ONESB[:, :], FLG[:, 1:2], 0.0, ALU.mult, ALU.add, ["ONESB", "FLG"], ["ONESR"])

    ZP = sbt([128, T], BF16)
    memset(ZP[:, :], 0.0, "ZP")
    dma("sp", XCH[320:KB_ROW, :], ZP[0:KB_ROW - 320, :], ["ZP"], [("XCH", 1)], "ZPst")
    dma("sp", XCH[KB_ROW + 1024:VB_ROW, :], ZP[0:VB_ROW - KB_ROW - 1024, :], ["ZP"], [("XCH", 5)], "ZPst")

    act(CACT[:, :], CT[:, :], AF.Silu, ["CT"], ["CACT"])
    ADP = sbt([128, DEPTH * 12], F32)
    MODR = sbt([128, DEPTH, 48], F32)
    for l in range(DEPTH):
        aw = ada_w[l].rearrange("(c p) n -> p c n", p=128)
        for g in range(6):
            s = next_ws()
            dma("pool", WS0[s][:, :, :], aw[:, :, g * 256:(g + 1) * 256], [], [("WS", s)], ("WS", s))
            for ct in range(2):
                jt = l * 12 + g * 2 + ct
                mm_group(PSB[7][:, jt:jt + 1],
                         [(WS0[s][:, c, ct * 128:(ct + 1) * 128], CACT[:, c:c + 1]) for c in range(16)],
                         [("WS", s), "CACT"], [("ps", 7)])
    cp("dve", ADP[:, :], PSB[7][:, 0:DEPTH * 12], [("ps", 7)], ["ADP"])
    dma("sp", ADX, ADP[:, :], ["ADP"], ["ADX"], "ADXst")
    SC.op("pool", lambda h: h.collective_compute("AllGather", ALU.bypass, replica_groups=[[0, 1, 2, 3], [4, 5, 6, 7]],
                                                 ins=[ADX], outs=[ADG]),
          reads=["ADX"], writes=["ADG"], dma_key="CCada", ndma=1, inc=1)
    dmas("sp", [(MODR[:, :, r * 12:(r + 1) * 12], ADG[r * 128:(r + 1) * 128, :].rearrange("p (l j) -> p l j", j=12)) for r in range(4)],
         ["ADG"], ["MODR"], "MODR")
    for l in range(DEPTH):
        tt(MOD[:, l, :], MODR[:, l, :], ADAB[:, l, :], ALU.add, ["MODR", "ADAB"], [("MOD", l)])
        stt(AMUL[:, l, :], MOD[:, l, 16:32], 1.0, NORMG[:, l, :], ALU.add, ALU.mult, [("MOD", l), "NORMG"], [("AMUL", l)])

    phase()
    COS = sbt([64, T], F32)
    SGN = sbt([64, T], F32)
    PI = sbt([64, T], I32)
    ANG = sbt([64, T], F32)
    TA = sbt([64, T], F32)
    TBf = sbt([64, T], F32)
    TI = sbt([64, T], I32)
    dma("sp", PI[:, :], bass.AP(pos.tensor, 0, [[0, 64], [1, T]]), [], ["PI"], "PI")
    cp("dve", ANG[:, :], PI[:, :], ["PI"], ["ANG"])
    ts(ANG[:, :], ANG[:, :], RC[:, 0:1], 0.0, ALU.mult, ALU.add, ["ANG", "RC"], ["ANG"])
    TWO_PI = 2.0 * math.pi
    C1 = 6.28125
    C2 = TWO_PI - C1
    PIC = 3.1415925

    def sin_of(shift, dst, dname):
        ts(TA[:, :], ANG[:, :], float(shift), 1.0 / TWO_PI, ALU.add, ALU.mult, ["ANG"], ["TA"])
        cp("dve", TI[:, :], TA[:, :], ["TA"], ["TI"])
        cp("dve", TA[:, :], TI[:, :], ["TI"], ["TA"])
        ts(TBf[:, :], ANG[:, :], float(shift), 0.0, ALU.add, ALU.add, ["ANG"], ["TBf"])
        stt(TBf[:, :], TA[:, :], -C1, TBf[:, :], ALU.mult, ALU.add, ["TA", "TBf"], ["TBf"])
        stt(TBf[:, :], TA[:, :], -C2, TBf[:, :], ALU.mult, ALU.add, ["TA", "TBf"], ["TBf"])
        ts(TBf[:, :], TBf[:, :], -PIC, PIC, ALU.max, ALU.min, ["TBf"], ["TBf"])
        act(dst[:, :], TBf[:, :], AF.Sin, ["TBf"], [dname])

    sin_of(0.0, SGN, "SGN")
    ts(SGN[:, :], SGN[:, :], RC[:, 1:2], 0.0, ALU.mult, ALU.add, ["SGN", "RC"], ["SGN"])
    sin_of(math.pi / 2, COS, "COS")
    dma("sp", COSd, COS[:, :], ["COS"], ["COSd"], "COSst")
    dma("sp", SGNd, SGN[:, :], ["SGN"], ["SGNd"], "SGNst")

    phase()
    LM = sbt([128, TWW], F32)
    BT = [sbt([128, TWW], F32) for _ in range(2)]
    TWb = [sbt([128, TWW], BF16) for _ in range(2)]
    dma("sp", LM[:, :], logm, [], ["LM"], "LM")
    for hd in range(NH):
        s = hd % 2
        dma("sp", BT[s][:, :], biasTw[hd], [], [("BT", s)], ("BT", s))
        tt(BT[s][:, :], BT[s][:, :], LM[:, :], ALU.add, [("BT", s), "LM"], [("BT", s)])
        act(TWb[s][:, :], BT[s][:, :], AF.Exp, [("BT", s)], [("TWb", s)])
        dma("sp", TW[hd], TWb[s][:, :], [("TWb", s)], [("TW", hd)], ("TWst", s))

    def rms_stats(srcs, src_names, ones_t, ones_nm, rstd_out, rstd_nm, sqs, tmp):
        n = len(srcs)
        for c in range(n):
            sq, sqn = sqs[c % 2]
            act(sq, srcs[c], AF.Square, [src_names[c]], [sqn])
            mm1(PSB[6][:, :], ones_t, sq, c == 0, c == n - 1, [sqn, ones_nm], [("ps", 6)])
        t, tn = tmp
        act(t, PSB[6][:, :], AF.Sqrt, [("ps", 6), "EPSC"], [tn], scale=1.0, bias=EPSC[:, 0:1])
        recip(rstd_out, t, [tn], [rstd_nm])

    def rstd_pass(xv, XL, SQ, TMP, RSTD):
        li = 0
        for tb in range(4):
            for g in range(4):
                s = li % 2
                li += 1
                dma("sp", XL[s][:, :, :], xv[:, 4 * g:4 * g + 4, tb * 512:(tb + 1) * 512],
                    [xt(4 * g + k, tb) for k in range(4)], [("XL", s)], ("XL", s))
                for k in range(4):
                    c = 4 * g + k
                    act(SQ[c % 2][:, :], XL[s][:, k, :], AF.Square, [("XL", s)], [("SQ", c % 2)])
                    mm1(PSB[6][:, :], ONESF[:, :], SQ[c % 2][:, :], c == 0, c == 15, [("SQ", c % 2), "ONESF"], [("ps", 6)])
            act(TMP[:, :], PSB[6][:, :], AF.Sqrt, [("ps", 6), "EPSC"], ["TMP"], scale=1.0, bias=EPSC[:, 0:1])
            recip(RSTD[:, tb * 512:(tb + 1) * 512], TMP[:, :], ["TMP"], [("RSTD", tb)])
        return li

    def attn_sweep(n, score_fn, post_fn, v_fn, ones_fn, fin_fn, PT, T01, T23, T4, ACC=None, extras=(), every=4):
        steps = [(qb, j) for qb in range(4) for j in range(n)]
        ptof = {}
        gof = {}
        LA = 4
        gcount = 0
        extras = list(extras)
        for i in range(len(steps) + LA + 2):
            k = i - LA
            if 0 <= k < len(steps):
                qb, j = steps[k]
                po = 5 + (qb % 2)
                lhsT, vreads = v_fn(qb, j)
                mm1(PSB[po][:, :], lhsT, PT[ptof[k]][:, :], j == 0, j == n - 1, vreads + [("PT", ptof[k])], [("ps", po)], nowait=True)
            k2 = i - LA - 2
            if 0 <= k2 < len(steps) and steps[k2][1] % 4 == 3:
                qb, j = steps[k2]
                po, pd = 5 + (qb % 2), 7
                g = gof[k2]
                if ACC is None:
                    ones_t, on = ones_fn(qb, j)
                    mm1(PSB[pd][:, :], ones_t[:, :], T4[g][:, :], j == 3, j == n - 1, [on, ("T4", g)], [("ps", pd)], nowait=True)
                elif j == n - 1:
                    mm1(PSB[pd][:, :], ONES1[:, :], ACC[qb % 2][:, :], True, True, ["ONES1", ("ACC", qb % 2)], [("ps", pd)])
                if j == n - 1:
                    fin_fn(qb, po, pd)
            if extras and i >= 8 and i % every == 2:
                extras.pop(0)()
            if i < len(steps):
                qb, j = steps[i]
                p = score_fn(qb, j)
                ptn = cnt["pt"] % 8
                cnt["pt"] += 1
                post_fn(qb, j, p, ptn)
                ptof[i] = ptn
                if j % 4 == 1:
                    g = gcount % 2
                    tt(T01[g][:, :], PT[ptof[i - 1]][:, :], PT[ptn][:, :], ALU.add, [("PT", ptof[i - 1]), ("PT", ptn)], [("T01", g)])
                if j % 4 == 3:
                    g = gcount % 2
                    gcount += 1
                    tt(T23[g][:, :], PT[ptof[i - 1]][:, :], PT[ptn][:, :], ALU.add, [("PT", ptof[i - 1]), ("PT", ptn)], [("T23", g)])
                    tt(T4[g][:, :], T01[g][:, :], T23[g][:, :], ALU.add, [("T01", g), ("T23", g)], [("T4", g)])
                    gof[i] = g
                    if ACC is not None:
                        a = qb % 2
                        if j == 3:
                            cp("dve", ACC[a][:, :], T4[g][:, :], [("T4", g)], [("ACC", a)])
                        else:
                            tt(ACC[a][:, :], ACC[a][:, :], T4[g][:, :], ALU.add, [("ACC", a), ("T4", g)], [("ACC", a)])
        while extras:
            extras.pop(0)()

    for l in range(depth):
        xsrc = xT if l == 0 else xres
        xsv = xsrc.rearrange("(c p) t -> p c t", p=128)
        xrv = xres.rearrange("(c p) t -> p c t", p=128)
        if stop < 1:
            break
        phase()
        XL = [sbt([128, 4, 512], F32) for _ in range(2)]
        SQ = [sbt([128, 512], F32) for _ in range(2)]
        TMP = sbt([128, 512], F32)
        RSTD = sbt([128, T], F32)
        XH = [sbt([128, 512], F32) for _ in range(2)]
        if l == 0:
            li = rstd_pass(xsv, XL, SQ, TMP, RSTD)
        else:
            li = 0
            for tb in range(4):
                dma("sp", RSTD[:, tb * 512:(tb + 1) * 512], RSd[:, tb * 512:(tb + 1) * 512], [("RSd", tb)], [("RSTD", tb)], ("RSld", tb))
        for tb in range(4):
            for g in range(4):
                s = li % 2
                li += 1
                dma("sp", XL[s][:, :, :], xsv[:, 4 * g:4 * g + 4, tb * 512:(tb + 1) * 512],
                    [xt(4 * g + k, tb) for k in range(4)], [("XL", s)], ("XL", s))
                for k in range(4):
                    c = 4 * g + k
                    xs = c % 2
                    tt(XH[xs][:, :], XL[s][:, k, :], RSTD[:, tb * 512:(tb + 1) * 512], ALU.mult,
                       [("XL", s), ("RSTD", tb)], [("XH", xs)])
                    act(HY[:, c, tb * 512:(tb + 1) * 512], XH[xs][:, :], AF.Identity, [("XH", xs), ("AMUL", l), ("MOD", l)],
                        [("HY", c, tb)], scale=AMUL[:, l, c:c + 1], bias=MOD[:, l, c:c + 1])

        if stop < 2:
            break
        phase()
        WS1 = [sbt([128, 16, 256], BF16) for _ in range(3)]
        OS = [sbt([128, 512], BF16) for _ in range(4)]
        SQb = [sbt([128, 512], F32) for _ in range(2)]
        TMPb = sbt([128, 512], F32)
        RS2 = sbt([128, 512], F32)
        CQ = sbt([128, 4, 512], F32)
        CQN = sbt([128, 4, T], BF16)
        CKVF = sbt([128, 2, 512], F32)
        WQ = sbt([128, 4, 1536], BF16)
        WQS = sbt([128, 4, 512], BF16)
        WKR = sbt([128, 16, 64], BF16)
        WKRS = sbt([128, 16, 64], BF16)
        R1 = sbt([64, 512], F32)
        R2 = sbt([64, 512], F32)
        COS = sbt([64, T], F32)
        SGN = sbt([64, T], F32)
        dma("sp", COS[:, :], COSd, ["COSd"], ["COS"], "COSld")
        dma("sp", SGN[:, :], SGNd, ["SGNd"], ["SGN"], "SGNld")
        sqs = [(SQb[0][:, :], ("SQb", 0)), (SQb[1][:, :], ("SQb", 1))]
        wv = w_in[l].rearrange("(c p) n -> p c n", p=128)

        def load_w(col0, ncols=256, WS1=WS1, wv=wv):
            s = next_ws()
            dma("pool", WS1[s][:, :, 0:ncols], wv[:, :, col0:col0 + ncols], [], [("WS", s)], ("WS", s))
            return s

        def proj_tile(s, ct, tb, WS1=WS1):
            p = next_ps()
            mm_group(PSB[p][:, :],
                     [(WS1[s][:, c, ct * 128:ct * 128 + 128], HY[:, c, tb * 512:(tb + 1) * 512]) for c in range(16)],
                     [("WS", s)] + hy_tb(tb), [("ps", p)])
            return p

        def rope_out(p1, p2, tb, o, COS=COS, SGN=SGN, R1=R1, R2=R2, OS=OS):
            tt(R1[:, :], PSB[p1][0:64, :], COS[:, tb * 512:(tb + 1) * 512], ALU.mult, [("ps", p1), "COS"], ["R1"])
            tt(R2[:, :], PSB[p2][0:64, :], SGN[:, tb * 512:(tb + 1) * 512], ALU.mult, [("ps", p2), "SGN"], ["R2"])
            tt(OS[o][0:64, :], R1[:, :], R2[:, :], ALU.add, ["R1", "R2"], [("OS", o)])

        s = load_w(C_IN["ckv"])
        for tb in range(4):
            for ct in range(2):
                p = proj_tile(s, ct, tb)
                cp("dve", CKVF[:, ct, :], PSB[p][:, :], [("ps", p)], [("CKVF", ct)])
            rms_stats([CKVF[:, c, :] for c in range(2)], [("CKVF", c) for c in range(2)], ONESK[:, :], "ONESK",
                      RS2[:, :], "RS2", sqs, (TMPb[:, :], "TMPb"))
            for ct in range(2):
                o = next_os()
                stt(OS[o][:, :], CKVF[:, ct, :], KVG[:, l, ct:ct + 1], RS2[:, :], ALU.mult, ALU.mult,
                    [("CKVF", ct), "RS2", "KVG"], [("OS", o)])
                dma("sp", XCH[ct * 128:(ct + 1) * 128, tb * 512:(tb + 1) * 512], OS[o][:, :], [("OS", o)], [("XCH", 0)], ("OSst", o))
        kr0 = C_IN["kr"]
        dmas("pool", [(WKR[:, :, :], wv[:, :, kr0:kr0 + 64]), (WKRS[:, :, 0:32], wv[:, :, kr0 + 32:kr0 + 64]),
                      (WKRS[:, :, 32:64], wv[:, :, kr0:kr0 + 32])], [], ["WKR"], "WKR")
        for tb in range(4):
            p1 = next_ps()
            mm_group(PSB[p1][0:64, :], [(WKR[:, c, :], HY[:, c, tb * 512:(tb + 1) * 512]) for c in range(16)],
                     ["WKR"] + hy_tb(tb), [("ps", p1)])
            p2 = next_ps()
            mm_group(PSB[p2][0:64, :], [(WKRS[:, c, :], HY[:, c, tb * 512:(tb + 1) * 512]) for c in range(16)],
                     ["WKR"] + hy_tb(tb), [("ps", p2)])
            o = next_os()
            rope_out(p1, p2, tb, o)
            dma("sp", XCH[256:320, tb * 512:(tb + 1) * 512], OS[o][0:64, :], [("OS", o)], [("XCH", 1)], ("OSst", o))
        for g in range(4):
            s = load_w(C_IN["kb"] + g * 256)
            for ct in range(2):
                hd = g * 2 + ct
                row = KB_ROW + hd * 128
                for tb in range(4):
                    p = proj_tile(s, ct, tb)
                    o = next_os()
                    cp(evac_eng(), OS[o][:, :], PSB[p][:, :], [("ps", p)], [("OS", o)])
                    dma("sp", XCH[row:row + 128, tb * 512:(tb + 1) * 512], OS[o][:, :], [("OS", o)], [("XCH", row // 256)], ("OSst", o))
        for g in range(4):
            s = load_w(C_IN["vb"] + g * 256)
            for tk in range(16):
                p = next_ps()
                mm_group(PSB[p][:, 0:256], [(HY[:, c, tk * 128:(tk + 1) * 128], WS1[s][:, c, 0:256]) for c in range(16)],
                         [("WS", s)] + hy_tb(tk // 4), [("ps", p)])
                o = next_os()
                cp(evac_eng(), OS[o][:, 0:256], PSB[p][:, 0:256], [("ps", p)], [("OS", o)])
                blk = (VB_ROW + tk * 64) // 256
                dma("sp", VBv[tk * 128:(tk + 1) * 128, g * 256:(g + 1) * 256], OS[o][:, 0:256], [("OS", o)], [("XCH", blk)], ("OSst", o))
        for k in range(NBLK):
            if _NOCC:
                dmas("pool", [(G[k * 1024 + r * 256:k * 1024 + (r + 1) * 256, :], XCH[k * 256:(k + 1) * 256, :]) for r in range(4)],
                     [("XCH", k)], [("G", k)], ("CC", k))
                continue
            SC.op("pool", lambda h, k=k: h.collective_compute(
                "AllGather", ALU.bypass, replica_groups=[[0, 1, 2, 3], [4, 5, 6, 7]],
                ins=[XCH[k * 256:(k + 1) * 256, :]], outs=[G[k * 1024:(k + 1) * 1024, :]]),
                reads=[("XCH", k)], writes=[("G", k)], dma_key=("CC", k), ndma=1, inc=1)
        s0 = load_w(0)
        s1 = load_w(256)
        for tb in range(4):
            for ct in range(4):
                p = proj_tile(s0 if ct < 2 else s1, ct % 2, tb)
                cp("dve", CQ[:, ct, :], PSB[p][:, :], [("ps", p)], [("CQ", ct)])
            rms_stats([CQ[:, c, :] for c in range(4)], [("CQ", c) for c in range(4)], ONESQ[:, :], "ONESQ",
                      RS2[:, :], "RS2", sqs, (TMPb[:, :], "TMPb"))
            for ct in range(4):
                stt(CQN[:, ct, tb * 512:(tb + 1) * 512], CQ[:, ct, :], QG[:, l, ct:ct + 1], RS2[:, :], ALU.mult, ALU.mult,
                    [("CQ", ct), "RS2", "QG"], [("CQN", tb)])
        for (sec, dst, silu) in (("ga", GA, True), ("qb", QB, False), ("gb", GB, True)):
            for g in range(4):
                s = load_w(C_IN[sec] + g * 256)
                for ct in range(2):
                    hd = g * 2 + ct
                    for tb in range(4):
                        p = proj_tile(s, ct, tb)
                        o = next_os()
                        if silu:
                            act(OS[o][:, :], PSB[p][:, :], AF.Silu, [("ps", p)], [("OS", o)])
                        else:
                            cp("dve", OS[o][:, :], PSB[p][:, :], [("ps", p)], [("OS", o)])
                        dma("sp", dst[hd, :, tb * 512:(tb + 1) * 512], OS[o][:, :], [("OS", o)], [(sec, hd)], ("OSst", o))
        wqv = w_q_up[l].rearrange("(c p) (h x) -> p c h x", p=128, x=192)
        wqs4 = WQS[:, :, :].rearrange("p c (h x) -> p c h x", x=64)
        dmas("pool", [(WQ[:, :, :], w_q_up[l].rearrange("(c p) n -> p c n", p=128))]
             + [(wqs4[:, c, :, 0:32], wqv[:, c, :, 160:192]) for c in range(4)]
             + [(wqs4[:, c, :, 32:64], wqv[:, c, :, 128:160]) for c in range(4)], [], ["WQ"], "WQ")
        for hd in range(NH):
            for tb in range(4):
                cq_r = [("CQN", tb), "WQ"]
                p = next_ps()
                mm_group(PSB[p][:, :], [(WQ[:, c, hd * 192:hd * 192 + 128], CQN[:, c, tb * 512:(tb + 1) * 512]) for c in range(4)],
                         cq_r, [("ps", p)])
                o = next_os()
                cp(evac_eng(), OS[o][:, :], PSB[p][:, :], [("ps", p)], [("OS", o)])
                dma("sp", QA[hd, 0:128, tb * 512:(tb + 1) * 512], OS[o][:, :], [("OS", o)], [("QA", hd)], ("OSst", o))
                p1 = next_ps()
                mm_group(PSB[p1][0:64, :], [(WQ[:, c, hd * 192 + 128:hd * 192 + 192], CQN[:, c, tb * 512:(tb + 1) * 512]) for c in range(4)],
                         cq_r, [("ps", p1)])
                p2 = next_ps()
                mm_group(PSB[p2][0:64, :], [(WQS[:, c, hd * 64:hd * 64 + 64], CQN[:, c, tb * 512:(tb + 1) * 512]) for c in range(4)],
                         cq_r, [("ps", p2)])
                o = next_os()
                rope_out(p1, p2, tb, o)
                dma("sp", QA[hd, 128:192, tb * 512:(tb + 1) * 512], OS[o][0:64, :], [("OS", o)], [("QA", hd)], ("OSst", o))

        if stop < 3:
            break
        phase()
        PT = [sbt([128, 512], BF16) for _ in range(8)]
        T01 = [sbt([128, 512], BF16) for _ in range(2)]
        T23 = [sbt([128, 512], BF16) for _ in range(2)]
        T4 = [sbt([128, 512], BF16) for _ in range(2)]
        ACC = [sbt([128, 512], F32) for _ in range(2)]
        RD = [sbt([128, 512], F32) for _ in range(1)] * 2
        YT = [sbt([128, 512], F32) for _ in range(1)] * 2
        KRG = sbt([64, 4, T], BF16)
        WKV = sbt([128, 2, 2048], BF16)
        KH2 = [sbt([128, S], BF16) for _ in range(2)]
        VH2 = [sbt([128, 64, 128], BF16) for _ in range(2)]
        LAT = [sbt([128, 2, 512], BF16) for _ in range(2)]
        QN = [sbt([128, T], BF16) for _ in range(2)]
        QR = [sbt([64, T], BF16) for _ in range(2)]
        GH = sbt([128, T], BF16)
        dma("sp", KRG[:, :, :], G3[4:8, 0:64, :].rearrange("r f t -> f r t"), [("G", 1)], ["KRG"], "KRG")
        dma("pool", WKV[:, :, :], w_kv_up[l].rearrange("(c p) n -> p c n", p=128), [], ["WKV"], "WKV")

        def finalize(po, pd, gate_ap, gate_reads, ychunk, qb, RD=RD, YT=YT):
            r = 0
            recip(RD[r][:, :], PSB[pd][:, :], [("ps", pd)], [("RD", r)])
            tt(YT[r][:, :], PSB[po][:, :], RD[r][:, :], ALU.mult, [("ps", po), ("RD", r)], [("YT", r)])
            tt(HY[:, ychunk, qb * 512:(qb + 1) * 512], YT[r][:, :], gate_ap, ALU.mult, [("YT", r)] + gate_reads, [("HY", ychunk, qb)])

        latc = [0]

        def prod_a(hd):
            nh = hd % 2
            out = []
            for kb in range(16):
                def unit(kb=kb, hd=hd, nh=nh):
                    r, tb = kb // 4, kb % 4
                    ls = latc[0] % 2
                    latc[0] += 1
                    dma("sp", LAT[ls][:, :, :], G3[r, :, tb * 512:(tb + 1) * 512].rearrange("(c p) t -> p c t", p=128),
                        [("G", 0)], [("LAT", ls)], ("LAT", ls))
                    p = next_ps()
                    mm_group(PSB[p][:, :], [(WKV[:, c, hd * 256:hd * 256 + 128], LAT[ls][:, c, :]) for c in range(2)],
                             ["WKV", ("LAT", ls)], [("ps", p)])
                    cp("dve", KH2[nh][:, kb * 512:(kb + 1) * 512], PSB[p][:, :], [("ps", p)], [("KH", nh, kb)])
                    p = next_ps()
                    for i4 in range(4):
                        mm_group(PSB[p][:, i4 * 128:(i4 + 1) * 128],
                                 [(LAT[ls][:, c, i4 * 128:(i4 + 1) * 128], WKV[:, c, hd * 256 + 128:hd * 256 + 256]) for c in range(2)],
                                 ["WKV", ("LAT", ls)], [("ps", p)])
                    cp("act", VH2[nh][:, kb * 4:(kb + 1) * 4, :], PSB[p][:, :].rearrange("p (a b) -> p a b", b=128),
                       [("ps", p)], [("VH", nh, kb)])
                out.append(unit)
            return out

        for f in prod_a(0):
            f()
        for hd in range(NH):
            hs = hd % 2
            dma("sp", QN[hs][:, :], QA[hd, 0:128, :], [("QA", hd)], [("QN", hs)], ("QN", hs))
            dma("sp", QR[hs][:, :], QA[hd, 128:192, :], [("QA", hd)], [("QR", hs)], ("QR", hs))
            dma("sp", GH[:, :], GA[hd], [("ga", hd)], ["GH"], "GH")
            nxt = prod_a(hd + 1) if hd + 1 < NH else []

            def score_a(qb, kt, hs=hs):
                r, tk = kt // 16, kt % 16
                p = next_ps(5)
                mm_group(PSB[p][:, :],
                         [(KH2[hs][:, kt * 128:(kt + 1) * 128], QN[hs][:, qb * 512:(qb + 1) * 512]),
                          (KRG[:, r, tk * 128:(tk + 1) * 128], QR[hs][:, qb * 512:(qb + 1) * 512])],
                         [("KH", hs, kt // 4), ("QN", hs), ("QR", hs), "KRG"], [("ps", p)], nowait=True)
                return p

            def post_a(qb, kt, p, ptn):
                act(PT[ptn][:, :], PSB[p][:, :], AF.Exp, [("ps", p)], [("PT", ptn)], scale=SC_A)

            attn_sweep(64, score_a, post_a, lambda qb, kt, hs=hs: (VH2[hs][:, kt, :], [("VH", hs, kt // 4)]),
                       lambda qb, kt: (ONESB, "ONESB"),
                       lambda qb, po, pd, hd=hd: finalize(po, pd, GH[:, qb * 512:(qb + 1) * 512], ["GH"], hd, qb),
                       PT, T01, T23, T4, ACC=ACC, extras=nxt, every=16)

        if stop < 4:
            break
        phase()
        PT = [sbt([128, 512], BF16) for _ in range(8)]
        T01 = [sbt([128, 512], BF16) for _ in range(2)]
        T23 = [sbt([128, 512], BF16) for _ in range(2)]
        T4 = [sbt([128, 512], BF16) for _ in range(2)]
        RD = [sbt([128, 512], F32) for _ in range(2)]
        YT = [sbt([128, 512], F32) for _ in range(2)]
        KBX = [sbt([128, 4096], BF16) for _ in range(2)]
        VBX = [sbt([128, 32, 128], BF16) for _ in range(2)]
        QBh = [sbt([128, T], BF16) for _ in range(2)]
        GBh = [sbt([128, T], BF16) for _ in range(2)]
        TWh = [sbt([128, TWW], BF16) for _ in range(2)]
        PTR = [sbt([128, 512], BF16) for _ in range(4)]

        KST = sbt([128, 2, 4, 1024], BF16)
        VST = sbt([128, 2, 4, 8, 128], BF16)

        def vtiles(src2d, tok0, ntok, hd):
            return src2d[tok0:tok0 + ntok, hd * 128:(hd + 1) * 128].rearrange("(n p) c -> p n c", p=128)

        def prep_b(hd):
            hs = hd % 2
            krow = KB_ROW + hd * 128
            kblk, koff = krow // 256, krow % 256
            dma("sp", KBX[hs][:, 1024:3072], XCH[krow:krow + 128, :], [("XCH", kblk)], [("KBX", hs, 1)], ("KBXo", hs))
            dma("sp", VBX[hs][:, 8:24, :], vtiles(VBv, 0, 2048, hd), [("XCH", b) for b in range(6, 10)], [("VBX", hs, 1)], ("VBXo", hs))
            dmas("sp", [(KST[:, 0, :, :], G3[kblk * 4:kblk * 4 + 4, koff:koff + 128, 1024:2048].rearrange("r f t -> f r t")),
                        (KST[:, 1, :, :], G3[kblk * 4:kblk * 4 + 4, koff:koff + 128, 0:1024].rearrange("r f t -> f r t"))],
                 [("G", kblk)], ["KST"], "KST")
            vp = []
            for r in range(4):
                for half in range(2):
                    for side, blk in ((0, 8 + half), (1, 6 + half)):
                        src = G3[blk * 4 + r, :, :].rearrange("f (two c) -> (f two) c", two=2)
                        vp.append((VST[:, side, r, half * 4:(half + 1) * 4, :], vtiles(src, 0, 512, hd)))
            dmas("sp", vp, [("G", b) for b in range(6, 10)], ["VST"], "VST")
            dma("sp", QBh[hs][:, :], QB[hd], [("qb", hd)], [("QBh", hs)], ("QBh", hs))
            dma("sp", GBh[hs][:, :], GB[hd], [("gb", hd)], [("GBh", hs)], ("GBh", hs))
            dma("sp", TWh[hs][:, :], TW[hd], [("TW", hd)], [("TWh", hs)], ("TWh", hs))
            sel = []
            for side, (dk, dv, nm) in enumerate(((KBX[hs][:, 0:1024], VBX[hs][:, 0:8, :], 0), (KBX[hs][:, 3072:4096], VBX[hs][:, 24:32, :], 2))):
                for r in range(4):
                    selc = FLG[:, 2 + side * 4 + r:3 + side * 4 + r]
                    if r == 0:
                        sel.append(lambda dk=dk, side=side, selc=selc, hs=hs, nm=nm: ts(
                            dk, KST[:, side, 0, :], selc, 0.0, ALU.mult, ALU.add, ["KST", "FLG"], [("KBX", hs, nm)]))
                        sel.append(lambda dv=dv, side=side, selc=selc, hs=hs, nm=nm: ts(
                            dv, VST[:, side, 0, :, :], selc, 0.0, ALU.mult, ALU.add, ["VST", "FLG"], [("VBX", hs, nm)]))
                    else:
                        sel.append(lambda dk=dk, side=side, r=r, selc=selc, hs=hs, nm=nm: stt(
                            dk, KST[:, side, r, :], selc, dk, ALU.mult, ALU.add, ["KST", "FLG", ("KBX", hs, nm)], [("KBX", hs, nm)]))
                        sel.append(lambda dv=dv, side=side, r=r, selc=selc, hs=hs, nm=nm: stt(
                            dv, VST[:, side, r, :, :], selc, dv, ALU.mult, ALU.add, ["VST", "FLG", ("VBX", hs, nm)], [("VBX", hs, nm)]))
            return sel

        for f in prep_b(0):
            f()
        for hd in range(NH):
            hs = hd % 2
            nxt = prep_b(hd + 1) if hd + 1 < NH else []

            def score_b(qb, i, hs=hs):
                kt = 4 * qb + i
                p = next_ps(5)
                mm_group(PSB[p][:, :], [(KBX[hs][:, kt * 128:(kt + 1) * 128], QBh[hs][:, qb * 512:(qb + 1) * 512])],
                         [("KBX", hs, 0), ("KBX", hs, 1), ("KBX", hs, 2), ("QBh", hs)], [("ps", p)], nowait=True)
                return p

            def post_b(qb, i, p, ptn, hs=hs):
                pr = ptn % 4
                act(PTR[pr][:, :], PSB[p][:, :], AF.Exp, [("ps", p)], [("PTR", pr)], scale=SC_B)
                tt(PT[ptn][:, :], PTR[pr][:, :], TWh[hs][:, (19 - i) * 128:(19 - i) * 128 + 512], ALU.mult,
                   [("PTR", pr), ("TWh", hs)], [("PT", ptn)])

            def ones_b(qb, i):
                kt = 4 * qb + i
                if kt < 8:
                    return ONESL, "ONESL"
                if kt >= 24:
                    return ONESR, "ONESR"
                return ONESB, "ONESB"

            attn_sweep(20, score_b, post_b,
                       lambda qb, i, hs=hs: (VBX[hs][:, 4 * qb + i, :], [("VBX", hs, 0), ("VBX", hs, 1), ("VBX", hs, 2)]),
                       ones_b,
                       lambda qb, po, pd, hd=hd, hs=hs: finalize(po, pd, GBh[hs][:, qb * 512:(qb + 1) * 512], [("GBh", hs)], 8 + hd, qb,
                                                                 RD=RD, YT=YT),
                       PT, T01, T23, T4, extras=nxt)

        if stop < 5:
            break
        phase()
        WS3 = [sbt([128, 16, 256], BF16) for _ in range(3)]
        XO = [sbt([128, 512], F32) for _ in range(3)]
        XN = [sbt([128, 512], F32) for _ in range(3)]
        SQ3 = [sbt([128, 512], BF16) for _ in range(2)]
        TMP3 = sbt([128, 512], F32)
        RS3 = [sbt([128, 512], F32) for _ in range(2)]
        wov = w_out[l].rearrange("(c p) n -> p c n", p=128)
        n3 = 0
        for g in range(8):
            s = next_ws()
            dma("pool", WS3[s][:, :, :], wov[:, :, g * 256:(g + 1) * 256], [], [("WS", s)], ("WS", s))
            for ct in range(2):
                dc = g * 2 + ct
                for tb in range(4):
                    xs = n3 % 3
                    n3 += 1
                    dma("sp", XO[xs][:, :], xsv[:, dc, tb * 512:(tb + 1) * 512], [xt(dc, tb)], [("XO", xs)], ("XO", xs))
                    p = next_ps()
                    mm_group(PSB[p][:, :], [(WS3[s][:, c, ct * 128:(ct + 1) * 128], HY[:, c, tb * 512:(tb + 1) * 512]) for c in range(16)],
                             [("WS", s)] + hy_tb(tb), [("ps", p)])
                    stt(XN[xs][:, :], PSB[p][:, :], MOD[:, l, 32 + dc:33 + dc], XO[xs][:, :], ALU.mult, ALU.add,
                        [("ps", p), ("XO", xs), ("MOD", l)], [("XN", xs)])
                    dma("sp", xrv[:, dc, tb * 512:(tb + 1) * 512], XN[xs][:, :], [("XN", xs)], [xt(dc, tb)], ("XNst", xs))
                    q3 = n3 % 2
                    act(SQ3[q3][:, :], XN[xs][:, :], AF.Square, [("XN", xs)], [("SQ3", q3)])
                    mm1(PSB[4 + tb][:, :], ONESFb[:, :], SQ3[q3][:, :], dc == 0, dc == 15, [("SQ3", q3), "ONESFb"], [("ps", 4 + tb)])
        for tb in range(4):
            act(TMP3[:, :], PSB[4 + tb][:, :], AF.Sqrt, [("ps", 4 + tb), "EPSC"], ["TMP3"], scale=1.0, bias=EPSC[:, 0:1])
            recip(RS3[tb % 2][:, :], TMP3[:, :], ["TMP3"], [("RS3", tb % 2)])
            dma("sp", RSd[:, tb * 512:(tb + 1) * 512], RS3[tb % 2][:, :], [("RS3", tb % 2)], [("RSd", tb)], ("RS3st", tb % 2))

    phase()
    XL = [sbt([128, 4, 512], F32) for _ in range(2)]
    SQ = [sbt([128, 512], F32) for _ in range(2)]
    TMP = sbt([128, 512], F32)
    RSTD = sbt([128, T], F32)
    XF = [sbt([128, 4, 512], F32) for _ in range(2)]
    xfv = (xres if (depth > 0 and stop >= 5) else xT).rearrange("(c p) t -> p c t", p=128)
    ov = outT.rearrange("(c p) t -> p c t", p=128)
    if depth > 0 and stop >= 5:
        li = 0
        for tb in range(4):
            dma("sp", RSTD[:, tb * 512:(tb + 1) * 512], RSd[:, tb * 512:(tb + 1) * 512], [("RSd", tb)], [("RSTD", tb)], ("RSld", tb))
    else:
        li = rstd_pass(xfv, XL, SQ, TMP, RSTD)
    outs = []
    for tb in range(4):
        for g in range(4):
            s = li % 2
            li += 1
            dma("sp", XL[s][:, :, :], xfv[:, 4 * g:4 * g + 4, tb * 512:(tb + 1) * 512],
                [xt(4 * g + k, tb) for k in range(4)], [("XL", s)], ("XL", s))
            for k in range(4):
                c = 4 * g + k
                stt(XF[s][:, k, :], XL[s][:, k, :], FNG[:, c:c + 1], RSTD[:, tb * 512:(tb + 1) * 512], ALU.mult, ALU.mult,
                    [("XL", s), ("RSTD", tb), "FNG"], [("XF", s, k)])
            outs.append(dma("sp", ov[:, 4 * g:4 * g + 4, tb * 512:(tb + 1) * 512], XF[s][:, :, :], [("XF", s, k) for k in range(4)],
                            [("out", g, tb)], ("XFst", s)))
    SC.final = outs
    SC.emit()
    return nc


def _t5_buckets(rel):
    nb = 16
    max_exact = nb // 2
    base = np.where(rel > 0, nb, 0)
    n = np.abs(rel)
    large = max_exact + (np.log(np.maximum(n, 1) / max_exact) / math.log(1024 / max_exact) * (nb - max_exact)).astype(np.int32)
    large = np.minimum(large, nb - 1)
    return (base + np.where(n < max_exact, n, large)).astype(np.int32)


_PROG = {}


def kernel(x, c, positions, norm_g, ada_w, ada_b, w_in, q_a_norm_g, w_q_up, kv_a_norm_g, w_kv_up, rel_bias, w_out,
           final_norm_g, _depth=DEPTH, _stop=99):
    f32 = np.float32
    x = np.asarray(x, f32)
    c = np.asarray(c, f32)
    positions = np.asarray(positions, np.int32)
    kk = np.arange(128)[:, None]
    nn = np.arange(TWW)[None, :]
    off = kk - nn + 1408
    mult = ((np.abs(off) <= 64).astype(np.int32) + ((np.abs(off) <= 256) & (off % 4 == 0)).astype(np.int32)
            + ((np.abs(off) <= 1024) & (off % 16 == 0)).astype(np.int32))
    logm = np.where(mult > 0, np.log(np.maximum(mult, 1)), -30000.0).astype(f32)
    bidx = _t5_buckets(np.clip(off, -1024, 1024))
    rb = np.asarray(rel_bias, f32)
    biasTw = np.ascontiguousarray(np.transpose(rb[bidx], (2, 0, 1)))
    inv = (1.0 / (10000.0 ** (np.arange(0, 64, 2, dtype=f32) / 64.0))).astype(f32)
    rconst = np.zeros((64, 2), f32)
    rconst[:, 0] = np.concatenate([inv, inv])
    rconst[:, 1] = np.concatenate([-np.ones(32, f32), np.ones(32, f32)])

    shared = {
        "normg": np.ascontiguousarray(np.asarray(norm_g, f32).reshape(DEPTH, 16, 128).transpose(2, 0, 1)),
        "adab": np.ascontiguousarray(np.asarray(ada_b, f32).reshape(DEPTH, 48, 128).transpose(2, 0, 1)),
        "qg": np.ascontiguousarray(np.asarray(q_a_norm_g, f32).reshape(DEPTH, 4, 128).transpose(2, 0, 1)),
        "kvg": np.ascontiguousarray(np.asarray(kv_a_norm_g, f32).reshape(DEPTH, 2, 128).transpose(2, 0, 1)),
        "fng": np.ascontiguousarray(np.asarray(final_norm_g, f32).reshape(16, 128).T),
        "rconst": rconst, "biasTw": biasTw, "logm": logm, "ident": np.eye(128, dtype=f32),
        "w_in": np.asarray(w_in, f32), "w_q_up": np.asarray(w_q_up, f32),
        "w_kv_up": np.asarray(w_kv_up, f32), "w_out": np.asarray(w_out, f32),
    }
    in_maps = []
    for core in range(8):
        b, j = core // 4, core % 4
        t0 = j * T
        m = dict(shared)
        m["xT"] = np.ascontiguousarray(x[b, t0:t0 + T, :].T)
        m["cT"] = np.ascontiguousarray(c[b].reshape(16, 128).T)
        m["pos"] = np.ascontiguousarray(positions[b, t0:t0 + T][None, :])
        fl = np.zeros((128, 10), f32)
        fl[:, 0] = 1.0 if j > 0 else 0.0
        fl[:, 1] = 1.0 if j < 3 else 0.0
        if j > 0:
            fl[:, 2 + (j - 1)] = 1.0
        if j < 3:
            fl[:, 6 + (j + 1)] = 1.0
        m["flags"] = fl
        m["ada_wp"] = np.ascontiguousarray(np.asarray(ada_w, f32)[:, :, j * 1536:(j + 1) * 1536])
        in_maps.append(m)
    if (_depth, _stop) not in _PROG:
        _PROG[(_depth, _stop)] = build_program(_depth, _stop)
    res = run_bass_kernel_spmd(_PROG[(_depth, _stop)], in_maps, core_ids=list(range(8)))
    out = np.empty((2, S, D), f32)
    for core in range(8):
        b, j = core // 4, core % 4
        out[b, j * T:(j + 1) * T, :] = np.asarray(res.results[core]["outT"]).T
    return out
```

```python
import contextlib
import math

import numpy as np
import concourse.bass as bass
import concourse.mybir as mybir
from concourse.bass_utils import run_bass_kernel_spmd

F32 = mybir.dt.float32
BF16 = mybir.dt.bfloat16
I32 = mybir.dt.int32
AF = mybir.ActivationFunctionType
ALU = mybir.AluOpType

ENGS = ("pe", "act", "dve", "pool", "sp")
_NOCC = False
_PEWAIT = False

D = 2048
T = 2048
S = 8192
DEPTH = 4
NH = 8
INW = 5952
EPS = 1e-6
NBLK = 10
XROWS = 256 * NBLK
KB_ROW = 384
VB_ROW = 1536
TWW = 2944
SC_A = 1.0 / math.sqrt(192.0)
SC_B = 1.0 / math.sqrt(128.0)
C_IN = dict(cq=0, ckv=512, kr=768, ga=832, qb=1856, kb=2880, vb=3904, gb=4928)


class Op:
    __slots__ = ("eng", "fn", "deps", "marked", "dma_key", "tok", "ndma", "inc", "nowait")

    def __init__(self, eng, fn, dma_key, ndma, inc=16):
        self.inc = inc
        self.nowait = False
        self.eng = eng
        self.fn = fn
        self.deps = ()
        self.marked = False
        self.dma_key = dma_key
        self.tok = 0
        self.ndma = ndma


class Sched:
    def __init__(self, nc):
        self.nc = nc
        self.ops = {e: [] for e in ENGS}
        self.bufs = {}
        self.dkeys = {}
        self.final = []
        self.last_dma = {}
        self.bar = None
        self.bar_passed = set()

    def barrier(self):
        deps = set()
        for e in ENGS:
            if self.ops[e]:
                deps.add(self.ops[e][-1])
        deps.update(self.last_dma.values())
        self.bar = deps
        self.bar_passed = set()

    def op(self, eng, fn, reads=(), writes=(), dma_key=None, ndma=1, inc=16, nowait=False):
        o = Op(eng, fn, dma_key, ndma, inc)
        o.nowait = nowait
        deps = set()
        bufs = self.bufs
        for b in reads:
            st = bufs.get(b)
            if st is None:
                st = bufs[b] = [None, []]
            if st[0] is not None:
                deps.add(st[0])
        for b in writes:
            st = bufs.get(b)
            if st is None:
                st = bufs[b] = [None, []]
            if st[0] is not None:
                deps.add(st[0])
            deps.update(st[1])
        for b in reads:
            bufs[b][1].append(o)
        for b in writes:
            st = bufs[b]
            st[0] = o
            st[1] = []
        if self.bar is not None and eng not in self.bar_passed:
            deps.update(self.bar)
            self.bar_passed.add(eng)
        for d in deps:
            if not (nowait and d.eng == "pe" and eng == "pe" and d.dma_key is None):
                d.marked = True
        o.deps = deps
        if dma_key is not None:
            k = self.dkeys.get(dma_key)
            if k is None:
                k = self.dkeys[dma_key] = [None, 0]
            k[1] += ndma * inc
            o.tok = k[1]
            self.last_dma[dma_key] = o
        self.ops[eng].append(o)
        return o

    def emit(self):
        nc = self.nc
        with contextlib.ExitStack() as es:
            esem = {e: es.enter_context(nc.semaphore("e_" + e)) for e in ENGS}
            for i, (k, v) in enumerate(self.dkeys.items()):
                v[0] = es.enter_context(nc.semaphore("d%d" % i))
            for d in self.final:
                d.marked = True
            for e in ENGS:
                n = 0
                for o in self.ops[e]:
                    if o.dma_key is None and o.marked:
                        n += 1
                        o.tok = n
            block = es.enter_context(nc.Block())
            dkeys = self.dkeys
            final = self.final

            def run(e, h):
                waited = {}
                for o in self.ops[e]:
                    need = {}
                    for d in o.deps:
                        if o.nowait and d.dma_key is None and d.eng == "pe" and e == "pe":
                            continue
                        sem = dkeys[d.dma_key][0] if d.dma_key is not None else esem[d.eng]
                        if waited.get(sem, 0) < d.tok and need.get(sem, 0) < d.tok:
                            need[sem] = d.tok
                    for sem, val in need.items():
                        h.wait_ge(sem, val)
                        waited[sem] = val
                    r = o.fn(h)
                    if o.dma_key is not None:
                        sem = dkeys[o.dma_key][0]
                        if not isinstance(r, (list, tuple)):
                            r = [r]
                        assert len(r) == o.ndma, (len(r), o.ndma)
                        for ins in r:
                            ins.then_inc(sem, o.inc)
                    elif o.marked:
                        r.then_inc(esem[e], 1)
                if e == "sp":
                    for d in final:
                        sem = dkeys[d.dma_key][0] if d.dma_key is not None else esem[d.eng]
                        h.wait_ge(sem, d.tok)

            @block.tensor
            def _(h):
                run("pe", h)

            @block.scalar
            def _(h):
                run("act", h)

            @block.vector
            def _(h):
                run("dve", h)

            @block.gpsimd
            def _(h):
                run("pool", h)

            @block.sync
            def _(h):
                run("sp", h)


def build_program(depth=DEPTH, stop=99):
    nc = bass.Bass("TRN2", target_bir_lowering=False)
    SC = Sched(nc)

    def din(name, shape, dt=F32):
        return nc.dram_tensor(name, shape, dt, kind="ExternalInput").ap()

    def dint(name, shape, dt=BF16):
        return nc.dram_tensor(name, shape, dt, kind="Internal").ap()

    xT = din("xT", [D, T])
    cT = din("cT", [128, 16])
    pos = din("pos", [1, T], I32)
    normg = din("normg", [128, DEPTH, 16])
    adab = din("adab", [128, DEPTH, 48])
    qg = din("qg", [128, DEPTH, 4])
    kvg = din("kvg", [128, DEPTH, 2])
    fng = din("fng", [128, 16])
    flags = din("flags", [128, 10])
    rconst = din("rconst", [64, 2])
    biasTw = din("biasTw", [NH, 128, TWW])
    logm = din("logm", [128, TWW])
    ident = din("ident", [128, 128])
    ada_w = din("ada_wp", [DEPTH, D, 1536])
    w_in = din("w_in", [DEPTH, D, INW])
    w_q_up = din("w_q_up", [DEPTH, 512, 1536])
    w_kv_up = din("w_kv_up", [DEPTH, 256, 2048])
    w_out = din("w_out", [DEPTH, D, D])
    outT = nc.dram_tensor("outT", [D, T], F32, kind="ExternalOutput").ap()

    xres = dint("xres", [D, T], F32)
    QA = dint("QA", [NH, 192, T])
    GA = dint("GA", [NH, 128, T])
    QB = dint("QB", [NH, 128, T])
    GB = dint("GB", [NH, 128, T])
    XCH = dint("XCH", [XROWS, T])
    G = dint("G", [NBLK * 4 * 256, T])
    TW = dint("TW", [NH, 128, TWW])
    ADX = dint("ADX", [128, DEPTH * 12], F32)
    ADG = dint("ADG", [4 * 128, DEPTH * 12], F32)
    RSd = dint("RSd", [128, T], F32)
    COSd = dint("COSd", [64, T], F32)
    SGNd = dint("SGNd", [64, T], F32)
    G3 = G.rearrange("(a f) t -> a f t", f=256)
    VBv = XCH[VB_ROW:VB_ROW + 1024, :].rearrange("r (two c) -> (r two) c", two=2)

    ARENA = 206 * 1024
    arena = nc.alloc_sbuf_tensor("arena", [128, ARENA], mybir.dt.uint8)
    base = nc.lookup_mloc(arena).addr
    cur = [0]
    names = [0]

    def sbt(shape, dt):
        nbytes = int(np.prod(shape[1:])) * (2 if dt == BF16 else 4)
        nbytes = (nbytes + 63) // 64 * 64
        at = cur[0]
        cur[0] += nbytes
        assert cur[0] <= ARENA, ("SBUF overflow", cur[0])
        names[0] += 1
        return nc.alloc_sbuf_tensor_at("t%d" % names[0], shape, dt, offset=base + at)

    HY = sbt([128, 16, T], BF16)
    MOD = sbt([128, DEPTH, 48], F32)
    AMUL = sbt([128, DEPTH, 16], F32)
    NORMG = sbt([128, DEPTH, 16], F32)
    QG = sbt([128, DEPTH, 4], F32)
    KVG = sbt([128, DEPTH, 2], F32)
    FNG = sbt([128, 16], F32)
    FLG = sbt([128, 10], F32)
    EPSC = sbt([128, 1], F32)
    ONESF = sbt([128, 128], F32)
    ONESQ = sbt([128, 128], F32)
    ONESK = sbt([128, 128], F32)
    ONESB = sbt([128, 128], BF16)
    ONESL = sbt([128, 128], BF16)
    ONESR = sbt([128, 128], BF16)
    CT = sbt([128, 16], F32)
    CACT = sbt([128, 16], BF16)
    ADAB = sbt([128, DEPTH, 48], F32)
    RC = sbt([64, 2], F32)
    IDB = sbt([128, 128], BF16)
    ONES1 = sbt([128, 128], F32)
    ONESFb = sbt([128, 128], BF16)
    ONESQb = sbt([128, 128], BF16)
    ONESKb = sbt([128, 128], BF16)
    phase_base = cur[0]

    def phase():
        SC.barrier()
        cur[0] = phase_base

    PSB = [nc.alloc_psum_tensor("ps%d" % i, [128, 512], F32) for i in range(8)]

    cnt = {"ps": 0, "os": 0, "ws": 0, "pt": 0, "ev": 0}

    def next_ps(n=4):
        i = cnt["ps"] % n
        cnt["ps"] += 1
        return i

    def next_os():
        i = cnt["os"] % 4
        cnt["os"] += 1
        return i

    def next_ws():
        i = cnt["ws"] % 3
        cnt["ws"] += 1
        return i

    def dma(q, out, in_, reads, writes, key):
        return SC.op(q, lambda h, o=out, i=in_: h.dma_start(out=o, in_=i), reads=reads, writes=writes, dma_key=key)

    def dmas(q, pairs, reads, writes, key):
        return SC.op(q, lambda h, pairs=pairs: [h.dma_start(out=o, in_=i) for (o, i) in pairs], reads=reads, writes=writes,
                     dma_key=key, ndma=len(pairs))

    def mm_group(out, pairs, reads, writes, nowait=False):
        n = len(pairs)

        def fn(h, out=out, pairs=pairs, n=n):
            r = None
            for i, (l, rr) in enumerate(pairs):
                r = h.matmul(out, lhsT=l, rhs=rr, start=(i == 0), stop=(i == n - 1))
            return r
        return SC.op("pe", fn, reads=reads, writes=writes, nowait=nowait)

    def mm1(out, lhsT, rhs, start, stop, reads, writes, nowait=False):
        return SC.op("pe", lambda h, o=out, l=lhsT, r=rhs, a=start, b=stop: h.matmul(o, lhsT=l, rhs=r, start=a, stop=b),
                     reads=reads, writes=writes, nowait=nowait and not _PEWAIT)

    def act(out, in_, func, reads, writes, scale=None, bias=None):
        kw = {}
        if scale is not None:
            kw["scale"] = scale
        if bias is not None:
            kw["bias"] = bias
        return SC.op("act", lambda h, o=out, i=in_, f=func, kw=kw: h.activation(out=o, in_=i, func=f, **kw), reads=reads, writes=writes)

    def tt(out, in0, in1, op, reads, writes, eng="dve"):
        return SC.op(eng, lambda h, o=out, a=in0, b=in1, op=op: h.tensor_tensor(out=o, in0=a, in1=b, op=op), reads=reads, writes=writes)

    def ts(out, in0, s1, s2, op0, op1, reads, writes, eng="dve"):
        return SC.op(eng, lambda h, o=out, a=in0, s1=s1, s2=s2, op0=op0, op1=op1: h.tensor_scalar(
            out=o, in0=a, scalar1=s1, scalar2=s2, op0=op0, op1=op1), reads=reads, writes=writes)

    def stt(out, in0, scalar, in1, op0, op1, reads, writes, eng="dve"):
        return SC.op(eng, lambda h, o=out, a=in0, sc=scalar, b=in1, op0=op0, op1=op1: h.scalar_tensor_tensor(
            out=o, in0=a, scalar=sc, in1=b, op0=op0, op1=op1), reads=reads, writes=writes)

    def recip(out, in_, reads, writes):
        return SC.op("dve", lambda h, o=out, i=in_: h.reciprocal(out=o, in_=i), reads=reads, writes=writes)

    def cp(eng, out, in_, reads, writes):
        if eng == "act":
            return act(out, in_, AF.Copy, reads, writes)
        return SC.op(eng, lambda h, o=out, i=in_: h.tensor_copy(out=o, in_=i), reads=reads, writes=writes)

    def memset(t, val, name):
        return SC.op("dve", lambda h, t=t, v=val: h.memset(t, v), writes=[name])

    def evac_eng():
        cnt["ev"] += 1
        return "act" if cnt["ev"] % 2 else "dve"

    rank_cache = {}

    def rank(h):
        return h.partition_id() % 4

    def xt(c, tb):
        return ("x", c, tb)

    def hy_tb(tb):
        return [("HY", c, tb) for c in range(16)]

    phase()
    WS0 = [sbt([128, 16, 256], BF16) for _ in range(3)]
    for (dst, src, nm) in ((NORMG, normg, "NORMG"), (QG, qg, "QG"), (KVG, kvg, "KVG"), (FNG, fng, "FNG"),
                           (FLG, flags, "FLG"), (CT, cT, "CT"), (ADAB, adab, "ADAB"), (RC, rconst, "RC")):
        dma("sp", dst[:], src, [], [nm], nm)
    dma("pool", IDB[:, :], ident, [], ["IDB"], "IDB")
    memset(EPSC[:, :], EPS, "EPSC")
    memset(ONESF[:, :], 1.0 / 2048, "ONESF")
    memset(ONESQ[:, :], 1.0 / 512, "ONESQ")
    memset(ONESK[:, :], 1.0 / 256, "ONESK")
    memset(ONESB[:, :], 1.0, "ONESB")
    memset(ONES1[:, :], 1.0, "ONES1")
    memset(ONESFb[:, :], 1.0 / 2048, "ONESFb")
    memset(ONESQb[:, :], 1.0 / 512, "ONESQb")
    memset(ONESKb[:, :], 1.0 / 256, "ONESKb")
    ts(ONESL[:, :], ONESB[:, :], FLG[:, 0:1], 0.0, ALU.mult, ALU.add, ["ONESB", "FLG"], ["ONESL"])
    ts(ONESR[:, :], ONESB[:, :], FLG[:, 1:2], 0.0, ALU.mult, ALU.add, ["ONESB", "FLG"], ["ONESR"])

    ZP = sbt([128, T], BF16)
    memset(ZP[:, :], 0.0, "ZP")
    dma("sp", XCH[320:KB_ROW, :], ZP[0:KB_ROW - 320, :], ["ZP"], [("XCH", 1)], "ZPst")
    dma("sp", XCH[KB_ROW + 1024:VB_ROW, :], ZP[0:VB_ROW - KB_ROW - 1024, :], ["ZP"], [("XCH", 5)], "ZPst")

    act(CACT[:, :], CT[:, :], AF.Silu, ["CT"], ["CACT"])
    ADP = sbt([128, DEPTH * 12], F32)
    MODR = sbt([128, DEPTH, 48], F32)
    for l in range(DEPTH):
        aw = ada_w[l].rearrange("(c p) n -> p c n", p=128)
        for g in range(6):
            s = next_ws()
            dma("pool", WS0[s][:, :, :], aw[:, :, g * 256:(g + 1) * 256], [], [("WS", s)], ("WS", s))
            for ct in range(2):
                jt = l * 12 + g * 2 + ct
                mm_group(PSB[7][:, jt:jt + 1],
                         [(WS0[s][:, c, ct * 128:(ct + 1) * 128], CACT[:, c:c + 1]) for c in range(16)],
                         [("WS", s), "CACT"], [("ps", 7)])
    cp("dve", ADP[:, :], PSB[7][:, 0:DEPTH * 12], [("ps", 7)], ["ADP"])
    dma("sp", ADX, ADP[:, :], ["ADP"], ["ADX"], "ADXst")
    SC.op("pool", lambda h: h.collective_compute("AllGather", ALU.bypass, replica_groups=[[0, 1, 2, 3], [4, 5, 6, 7]],
                                                 ins=[ADX], outs=[ADG]),
          reads=["ADX"], writes=["ADG"], dma_key="CCada", ndma=1, inc=1)
    dmas("sp", [(MODR[:, :, r * 12:(r + 1) * 12], ADG[r * 128:(r + 1) * 128, :].rearrange("p (l j) -> p l j", j=12)) for r in range(4)],
         ["ADG"], ["MODR"], "MODR")
    for l in range(DEPTH):
        tt(MOD[:, l, :], MODR[:, l, :], ADAB[:, l, :], ALU.add, ["MODR", "ADAB"], [("MOD", l)])
        stt(AMUL[:, l, :], MOD[:, l, 16:32], 1.0, NORMG[:, l, :], ALU.add, ALU.mult, [("MOD", l), "NORMG"], [("AMUL", l)])

    phase()
    COS = sbt([64, T], F32)
    SGN = sbt([64, T], F32)
    PI = sbt([64, T], I32)
    ANG = sbt([64, T], F32)
    TA = sbt([64, T], F32)
    TBf = sbt([64, T], F32)
    TI = sbt([64, T], I32)
    dma("sp", PI[:, :], bass.AP(pos.tensor, 0, [[0, 64], [1, T]]), [], ["PI"], "PI")
    cp("dve", ANG[:, :], PI[:, :], ["PI"], ["ANG"])
    ts(ANG[:, :], ANG[:, :], RC[:, 0:1], 0.0, ALU.mult, ALU.add, ["ANG", "RC"], ["ANG"])
    TWO_PI = 2.0 * math.pi
    C1 = 6.28125
    C2 = TWO_PI - C1
    PIC = 3.1415925

    def sin_of(shift, dst, dname):
        ts(TA[:, :], ANG[:, :], float(shift), 1.0 / TWO_PI, ALU.add, ALU.mult, ["ANG"], ["TA"])
        cp("dve", TI[:, :], TA[:, :], ["TA"], ["TI"])
        cp("dve", TA[:, :], TI[:, :], ["TI"], ["TA"])
        ts(TBf[:, :], ANG[:, :], float(shift), 0.0, ALU.add, ALU.add, ["ANG"], ["TBf"])
        stt(TBf[:, :], TA[:, :], -C1, TBf[:, :], ALU.mult, ALU.add, ["TA", "TBf"], ["TBf"])
        stt(TBf[:, :], TA[:, :], -C2, TBf[:, :], ALU.mult, ALU.add, ["TA", "TBf"], ["TBf"])
        ts(TBf[:, :], TBf[:, :], -PIC, PIC, ALU.max, ALU.min, ["TBf"], ["TBf"])
        act(dst[:, :], TBf[:, :], AF.Sin, ["TBf"], [dname])

    sin_of(0.0, SGN, "SGN")
    ts(SGN[:, :], SGN[:, :], RC[:, 1:2], 0.0, ALU.mult, ALU.add, ["SGN", "RC"], ["SGN"])
    sin_of(math.pi / 2, COS, "COS")
    dma("sp", COSd, COS[:, :], ["COS"], ["COSd"], "COSst")
    dma("sp", SGNd, SGN[:, :], ["SGN"], ["SGNd"], "SGNst")

    phase()
    LM = sbt([128, TWW], F32)
    BT = [sbt([128, TWW], F32) for _ in range(2)]
    TWb = [sbt([128, TWW], BF16) for _ in range(2)]
    dma("sp", LM[:, :], logm, [], ["LM"], "LM")
    for hd in range(NH):
        s = hd % 2
        dma("sp", BT[s][:, :], biasTw[hd], [], [("BT", s)], ("BT", s))
        tt(BT[s][:, :], BT[s][:, :], LM[:, :], ALU.add, [("BT", s), "LM"], [("BT", s)])
        ts(TWb[s][:, :], BT[s][:, :], 1.0 / SC_B, 0.0, ALU.mult, ALU.add, [("BT", s)], [("TWb", s)])
        dma("sp", TW[hd], TWb[s][:, :], [("TWb", s)], [("TW", hd)], ("TWst", s))

    def rms_stats(srcs, src_names, ones_t, ones_nm, rstd_out, rstd_nm, sqs, tmp):
        n = len(srcs)
        for c in range(n):
            sq, sqn = sqs[c % 2]
            act(sq, srcs[c], AF.Square, [src_names[c]], [sqn])
            mm1(PSB[6][:, :], ones_t, sq, c == 0, c == n - 1, [sqn, ones_nm], [("ps", 6)])
        t, tn = tmp
        act(t, PSB[6][:, :], AF.Sqrt, [("ps", 6), "EPSC"], [tn], scale=1.0, bias=EPSC[:, 0:1])
        recip(rstd_out, t, [tn], [rstd_nm])

    def rstd_pass(xv, XL, SQ, TMP, RSTD):
        li = 0
        for tb in range(4):
            for g in range(4):
                s = li % 2
                li += 1
                dma("sp", XL[s][:, :, :], xv[:, 4 * g:4 * g + 4, tb * 512:(tb + 1) * 512],
                    [xt(4 * g + k, tb) for k in range(4)], [("XL", s)], ("XL", s))
                for k in range(4):
                    c = 4 * g + k
                    act(SQ[c % 2][:, :], XL[s][:, k, :], AF.Square, [("XL", s)], [("SQ", c % 2)])
                    mm1(PSB[6][:, :], ONESFb[:, :], SQ[c % 2][:, :], c == 0, c == 15, [("SQ", c % 2), "ONESFb"], [("ps", 6)])
            act(TMP[:, :], PSB[6][:, :], AF.Sqrt, [("ps", 6), "EPSC"], ["TMP"], scale=1.0, bias=EPSC[:, 0:1])
            recip(RSTD[:, tb * 512:(tb + 1) * 512], TMP[:, :], ["TMP"], [("RSTD", tb)])
        return li

    def attn_sweep(n, score_fn, post_fn, v_fn, ones_fn, fin_fn, PT, T01, T23, T4, ACC=None, extras=(), every=4):
        steps = [(qb, j) for qb in range(4) for j in range(n)]
        ptof = {}
        gof = {}
        LA = 4
        gcount = 0
        extras = list(extras)
        for i in range(len(steps) + LA + 2):
            k = i - LA
            if 0 <= k < len(steps):
                qb, j = steps[k]
                po = 5 + (qb % 2)
                lhsT, vreads = v_fn(qb, j)
                mm1(PSB[po][:, :], lhsT, PT[ptof[k]][:, :], j == 0, j == n - 1, vreads + [("PT", ptof[k])], [("ps", po)], nowait=True)
            k2 = i - LA - 2
            if 0 <= k2 < len(steps) and steps[k2][1] % 4 == 3:
                qb, j = steps[k2]
                po, pd = 5 + (qb % 2), 7
                g = gof[k2]
                if ACC is None:
                    ones_t, on = ones_fn(qb, j)
                    mm1(PSB[pd][:, :], ones_t[:, :], T4[g][:, :], j == 3, j == n - 1, [on, ("T4", g)], [("ps", pd)], nowait=True)
                elif j == n - 1:
                    mm1(PSB[pd][:, :], ONES1[:, :], ACC[qb % 2][:, :], True, True, ["ONES1", ("ACC", qb % 2)], [("ps", pd)])
                if j == n - 1:
                    fin_fn(qb, po, pd)
            if extras and i >= 8 and i % every == 2:
                extras.pop(0)()
            if i < len(steps):
                qb, j = steps[i]
                p = score_fn(qb, j)
                ptn = cnt["pt"] % 8
                cnt["pt"] += 1
                post_fn(qb, j, p, ptn)
                ptof[i] = ptn
                if j % 4 == 1:
                    g = gcount % 2
                    tt(T01[g][:, :], PT[ptof[i - 1]][:, :], PT[ptn][:, :], ALU.add, [("PT", ptof[i - 1]), ("PT", ptn)], [("T01", g)])
                if j % 4 == 3:
                    g = gcount % 2
                    gcount += 1
                    tt(T23[g][:, :], PT[ptof[i - 1]][:, :], PT[ptn][:, :], ALU.add, [("PT", ptof[i - 1]), ("PT", ptn)], [("T23", g)])
                    tt(T4[g][:, :], T01[g][:, :], T23[g][:, :], ALU.add, [("T01", g), ("T23", g)], [("T4", g)])
                    gof[i] = g
                    if ACC is not None:
                        a = qb % 2
                        if j == 3:
                            cp("dve", ACC[a][:, :], T4[g][:, :], [("T4", g)], [("ACC", a)])
                        else:
                            tt(ACC[a][:, :], ACC[a][:, :], T4[g][:, :], ALU.add, [("ACC", a), ("T4", g)], [("ACC", a)])
        while extras:
            extras.pop(0)()

    for l in range(depth):
        xsrc = xT if l == 0 else xres
        xsv = xsrc.rearrange("(c p) t -> p c t", p=128)
        xrv = xres.rearrange("(c p) t -> p c t", p=128)
        if stop < 1:
            break
        phase()
        XL = [sbt([128, 4, 512], F32) for _ in range(2)]
        SQ = [sbt([128, 512], BF16) for _ in range(2)]
        TMP = sbt([128, 512], F32)
        RSTD = sbt([128, T], F32)
        XH = [sbt([128, 512], F32) for _ in range(2)]
        if l == 0:
            li = rstd_pass(xsv, XL, SQ, TMP, RSTD)
        else:
            li = 0
            for tb in range(4):
                dma("sp", RSTD[:, tb * 512:(tb + 1) * 512], RSd[:, tb * 512:(tb + 1) * 512], [("RSd", tb)], [("RSTD", tb)], ("RSld", tb))
        for tb in range(4):
            for g in range(4):
                s = li % 2
                li += 1
                dma("sp", XL[s][:, :, :], xsv[:, 4 * g:4 * g + 4, tb * 512:(tb + 1) * 512],
                    [xt(4 * g + k, tb) for k in range(4)], [("XL", s)], ("XL", s))
                for k in range(4):
                    c = 4 * g + k
                    xs = c % 2
                    tt(XH[xs][:, :], XL[s][:, k, :], RSTD[:, tb * 512:(tb + 1) * 512], ALU.mult,
                       [("XL", s), ("RSTD", tb)], [("XH", xs)])
                    act(HY[:, c, tb * 512:(tb + 1) * 512], XH[xs][:, :], AF.Identity, [("XH", xs), ("AMUL", l), ("MOD", l)],
                        [("HY", c, tb)], scale=AMUL[:, l, c:c + 1], bias=MOD[:, l, c:c + 1])

        if stop < 2:
            break
        phase()
        WS1 = [sbt([128, 16, 256], BF16) for _ in range(3)]
        OS = [sbt([128, 512], BF16) for _ in range(4)]
        SQb = [sbt([128, 512], BF16) for _ in range(2)]
        TMPb = sbt([128, 512], F32)
        RS2 = sbt([128, 512], F32)
        CQ = sbt([128, 4, 512], F32)
        CQN = sbt([128, 4, T], BF16)
        CKVF = sbt([128, 2, 512], F32)
        WQ = sbt([128, 4, 1536], BF16)
        WQS = sbt([128, 4, 512], BF16)
        WKR = sbt([128, 16, 64], BF16)
        WKRS = sbt([128, 16, 64], BF16)
        R1 = sbt([64, 512], F32)
        R2 = sbt([64, 512], F32)
        COS = sbt([64, T], F32)
        SGN = sbt([64, T], F32)
        dma("sp", COS[:, :], COSd, ["COSd"], ["COS"], "COSld")
        dma("sp", SGN[:, :], SGNd, ["SGNd"], ["SGN"], "SGNld")
        sqs = [(SQb[0][:, :], ("SQb", 0)), (SQb[1][:, :], ("SQb", 1))]
        wv = w_in[l].rearrange("(c p) n -> p c n", p=128)

        def load_w(col0, ncols=256, WS1=WS1, wv=wv):
            s = next_ws()
            dma("pool", WS1[s][:, :, 0:ncols], wv[:, :, col0:col0 + ncols], [], [("WS", s)], ("WS", s))
            return s

        def proj_tile(s, ct, tb, WS1=WS1):
            p = next_ps()
            mm_group(PSB[p][:, :],
                     [(WS1[s][:, c, ct * 128:ct * 128 + 128], HY[:, c, tb * 512:(tb + 1) * 512]) for c in range(16)],
                     [("WS", s)] + hy_tb(tb), [("ps", p)])
            return p

        def rope_out(p1, p2, tb, o, COS=COS, SGN=SGN, R1=R1, R2=R2, OS=OS):
            tt(R1[:, :], PSB[p1][0:64, :], COS[:, tb * 512:(tb + 1) * 512], ALU.mult, [("ps", p1), "COS"], ["R1"])
            tt(R2[:, :], PSB[p2][0:64, :], SGN[:, tb * 512:(tb + 1) * 512], ALU.mult, [("ps", p2), "SGN"], ["R2"])
            tt(OS[o][0:64, :], R1[:, :], R2[:, :], ALU.add, ["R1", "R2"], [("OS", o)])

        s = load_w(C_IN["ckv"])
        for tb in range(4):
            for ct in range(2):
                p = proj_tile(s, ct, tb)
                cp("dve", CKVF[:, ct, :], PSB[p][:, :], [("ps", p)], [("CKVF", ct)])
            rms_stats([CKVF[:, c, :] for c in range(2)], [("CKVF", c) for c in range(2)], ONESKb[:, :], "ONESKb",
                      RS2[:, :], "RS2", sqs, (TMPb[:, :], "TMPb"))
            for ct in range(2):
                o = next_os()
                stt(OS[o][:, :], CKVF[:, ct, :], KVG[:, l, ct:ct + 1], RS2[:, :], ALU.mult, ALU.mult,
                    [("CKVF", ct), "RS2", "KVG"], [("OS", o)])
                dma("sp", XCH[ct * 128:(ct + 1) * 128, tb * 512:(tb + 1) * 512], OS[o][:, :], [("OS", o)], [("XCH", 0)], ("OSst", o))
        kr0 = C_IN["kr"]
        dmas("pool", [(WKR[:, :, :], wv[:, :, kr0:kr0 + 64]), (WKRS[:, :, 0:32], wv[:, :, kr0 + 32:kr0 + 64]),
                      (WKRS[:, :, 32:64], wv[:, :, kr0:kr0 + 32])], [], ["WKR"], "WKR")
        for tb in range(4):
            p1 = next_ps()
            mm_group(PSB[p1][0:64, :], [(WKR[:, c, :], HY[:, c, tb * 512:(tb + 1) * 512]) for c in range(16)],
                     ["WKR"] + hy_tb(tb), [("ps", p1)])
            p2 = next_ps()
            mm_group(PSB[p2][0:64, :], [(WKRS[:, c, :], HY[:, c, tb * 512:(tb + 1) * 512]) for c in range(16)],
                     ["WKR"] + hy_tb(tb), [("ps", p2)])
            o = next_os()
            rope_out(p1, p2, tb, o)
            dma("sp", XCH[256:320, tb * 512:(tb + 1) * 512], OS[o][0:64, :], [("OS", o)], [("XCH", 1)], ("OSst", o))
        for g in range(4):
            s = load_w(C_IN["kb"] + g * 256)
            for ct in range(2):
                hd = g * 2 + ct
                row = KB_ROW + hd * 128
                for tb in range(4):
                    p = proj_tile(s, ct, tb)
                    o = next_os()
                    cp(evac_eng(), OS[o][:, :], PSB[p][:, :], [("ps", p)], [("OS", o)])
                    dma("sp", XCH[row:row + 128, tb * 512:(tb + 1) * 512], OS[o][:, :], [("OS", o)], [("XCH", row // 256)], ("OSst", o))
        for g in range(4):
            s = load_w(C_IN["vb"] + g * 256)
            for tk in range(16):
                p = next_ps()
                mm_group(PSB[p][:, 0:256], [(HY[:, c, tk * 128:(tk + 1) * 128], WS1[s][:, c, 0:256]) for c in range(16)],
                         [("WS", s)] + hy_tb(tk // 4), [("ps", p)])
                o = next_os()
                cp(evac_eng(), OS[o][:, 0:256], PSB[p][:, 0:256], [("ps", p)], [("OS", o)])
                blk = (VB_ROW + tk * 64) // 256
                dma("sp", VBv[tk * 128:(tk + 1) * 128, g * 256:(g + 1) * 256], OS[o][:, 0:256], [("OS", o)], [("XCH", blk)], ("OSst", o))
        for k in range(NBLK):
            if _NOCC:
                dmas("pool", [(G[k * 1024 + r * 256:k * 1024 + (r + 1) * 256, :], XCH[k * 256:(k + 1) * 256, :]) for r in range(4)],
                     [("XCH", k)], [("G", k)], ("CC", k))
                continue
            SC.op("pool", lambda h, k=k: h.collective_compute(
                "AllGather", ALU.bypass, replica_groups=[[0, 1, 2, 3], [4, 5, 6, 7]],
                ins=[XCH[k * 256:(k + 1) * 256, :]], outs=[G[k * 1024:(k + 1) * 1024, :]]),
                reads=[("XCH", k)], writes=[("G", k)], dma_key=("CC", k), ndma=1, inc=1)
        s0 = load_w(0)
        s1 = load_w(256)
        for tb in range(4):
            for ct in range(4):
                p = proj_tile(s0 if ct < 2 else s1, ct % 2, tb)
                cp("dve", CQ[:, ct, :], PSB[p][:, :], [("ps", p)], [("CQ", ct)])
            rms_stats([CQ[:, c, :] for c in range(4)], [("CQ", c) for c in range(4)], ONESQb[:, :], "ONESQb",
                      RS2[:, :], "RS2", sqs, (TMPb[:, :], "TMPb"))
            for ct in range(4):
                stt(CQN[:, ct, tb * 512:(tb + 1) * 512], CQ[:, ct, :], QG[:, l, ct:ct + 1], RS2[:, :], ALU.mult, ALU.mult,
                    [("CQ", ct), "RS2", "QG"], [("CQN", tb)])
        for (sec, dst, silu) in (("ga", GA, True), ("qb", QB, False), ("gb", GB, True)):
            for g in range(4):
                s = load_w(C_IN[sec] + g * 256)
                for ct in range(2):
                    hd = g * 2 + ct
                    for tb in range(4):
                        p = proj_tile(s, ct, tb)
                        o = next_os()
                        if silu:
                            act(OS[o][:, :], PSB[p][:, :], AF.Silu, [("ps", p)], [("OS", o)])
                        else:
                            cp("dve", OS[o][:, :], PSB[p][:, :], [("ps", p)], [("OS", o)])
                        dma("sp", dst[hd, :, tb * 512:(tb + 1) * 512], OS[o][:, :], [("OS", o)], [(sec, hd)], ("OSst", o))
        wqv = w_q_up[l].rearrange("(c p) (h x) -> p c h x", p=128, x=192)
        wqs4 = WQS[:, :, :].rearrange("p c (h x) -> p c h x", x=64)
        dmas("pool", [(WQ[:, :, :], w_q_up[l].rearrange("(c p) n -> p c n", p=128))]
             + [(wqs4[:, c, :, 0:32], wqv[:, c, :, 160:192]) for c in range(4)]
             + [(wqs4[:, c, :, 32:64], wqv[:, c, :, 128:160]) for c in range(4)], [], ["WQ"], "WQ")
        for hd in range(NH):
            for tb in range(4):
                cq_r = [("CQN", tb), "WQ"]
                p = next_ps()
                mm_group(PSB[p][:, :], [(WQ[:, c, hd * 192:hd * 192 + 128], CQN[:, c, tb * 512:(tb + 1) * 512]) for c in range(4)],
                         cq_r, [("ps", p)])
                o = next_os()
                cp(evac_eng(), OS[o][:, :], PSB[p][:, :], [("ps", p)], [("OS", o)])
                dma("sp", QA[hd, 0:128, tb * 512:(tb + 1) * 512], OS[o][:, :], [("OS", o)], [("QA", hd)], ("OSst", o))
                p1 = next_ps()
                mm_group(PSB[p1][0:64, :], [(WQ[:, c, hd * 192 + 128:hd * 192 + 192], CQN[:, c, tb * 512:(tb + 1) * 512]) for c in range(4)],
                         cq_r, [("ps", p1)])
                p2 = next_ps()
                mm_group(PSB[p2][0:64, :], [(WQS[:, c, hd * 64:hd * 64 + 64], CQN[:, c, tb * 512:(tb + 1) * 512]) for c in range(4)],
                         cq_r, [("ps", p2)])
                o = next_os()
                rope_out(p1, p2, tb, o)
                dma("sp", QA[hd, 128:192, tb * 512:(tb + 1) * 512], OS[o][0:64, :], [("OS", o)], [("QA", hd)], ("OSst", o))

        if stop < 3:
            break
        phase()
        PT = [sbt([128, 512], BF16) for _ in range(8)]
        T01 = [sbt([128, 512], BF16) for _ in range(2)]
        T23 = [sbt([128, 512], BF16) for _ in range(2)]
        T4 = [sbt([128, 512], BF16) for _ in range(2)]
        ACC = [sbt([128, 512], F32) for _ in range(2)]
        RD = [sbt([128, 512], F32) for _ in range(1)] * 2
        YT = [sbt([128, 512], F32) for _ in range(1)] * 2
        KRG = sbt([64, 4, T], BF16)
        WKV = sbt([128, 2, 2048], BF16)
        KH2 = [sbt([128, S], BF16) for _ in range(2)]
        VH2 = [sbt([128, 64, 128], BF16) for _ in range(2)]
        LAT = [sbt([128, 2, 512], BF16) for _ in range(2)]
        QN = [sbt([128, T], BF16) for _ in range(2)]
        QR = [sbt([64, T], BF16) for _ in range(2)]
        GH = sbt([128, T], BF16)
        dma("sp", KRG[:, :, :], G3[4:8, 0:64, :].rearrange("r f t -> f r t"), [("G", 1)], ["KRG"], "KRG")
        dma("pool", WKV[:, :, :], w_kv_up[l].rearrange("(c p) n -> p c n", p=128), [], ["WKV"], "WKV")

        def finalize(po, pd, gate_ap, gate_reads, ychunk, qb, RD=RD, YT=YT):
            r = 0
            recip(RD[r][:, :], PSB[pd][:, :], [("ps", pd)], [("RD", r)])
            tt(YT[r][:, :], PSB[po][:, :], RD[r][:, :], ALU.mult, [("ps", po), ("RD", r)], [("YT", r)])
            tt(HY[:, ychunk, qb * 512:(qb + 1) * 512], YT[r][:, :], gate_ap, ALU.mult, [("YT", r)] + gate_reads, [("HY", ychunk, qb)])

        latc = [0]

        def prod_a(hd):
            nh = hd % 2
            out = []
            for kb in range(16):
                def unit(kb=kb, hd=hd, nh=nh):
                    r, tb = kb // 4, kb % 4
                    ls = latc[0] % 2
                    latc[0] += 1
                    dma("sp", LAT[ls][:, :, :], G3[r, :, tb * 512:(tb + 1) * 512].rearrange("(c p) t -> p c t", p=128),
                        [("G", 0)], [("LAT", ls)], ("LAT", ls))
                    p = next_ps()
                    mm_group(PSB[p][:, :], [(WKV[:, c, hd * 256:hd * 256 + 128], LAT[ls][:, c, :]) for c in range(2)],
                             ["WKV", ("LAT", ls)], [("ps", p)])
                    cp("dve", KH2[nh][:, kb * 512:(kb + 1) * 512], PSB[p][:, :], [("ps", p)], [("KH", nh, kb)])
                    p = next_ps()
                    for i4 in range(4):
                        mm_group(PSB[p][:, i4 * 128:(i4 + 1) * 128],
                                 [(LAT[ls][:, c, i4 * 128:(i4 + 1) * 128], WKV[:, c, hd * 256 + 128:hd * 256 + 256]) for c in range(2)],
                                 ["WKV", ("LAT", ls)], [("ps", p)])
                    cp("act", VH2[nh][:, kb * 4:(kb + 1) * 4, :], PSB[p][:, :].rearrange("p (a b) -> p a b", b=128),
                       [("ps", p)], [("VH", nh, kb)])
                out.append(unit)
            return out

        for f in prod_a(0):
            f()
        for hd in range(NH):
            hs = hd % 2
            dma("sp", QN[hs][:, :], QA[hd, 0:128, :], [("QA", hd)], [("QN", hs)], ("QN", hs))
            dma("sp", QR[hs][:, :], QA[hd, 128:192, :], [("QA", hd)], [("QR", hs)], ("QR", hs))
            dma("sp", GH[:, :], GA[hd], [("ga", hd)], ["GH"], "GH")
            nxt = prod_a(hd + 1) if hd + 1 < NH else []

            def score_a(qb, kt, hs=hs):
                r, tk = kt // 16, kt % 16
                p = next_ps(5)
                mm_group(PSB[p][:, :],
                         [(KH2[hs][:, kt * 128:(kt + 1) * 128], QN[hs][:, qb * 512:(qb + 1) * 512]),
                          (KRG[:, r, tk * 128:(tk + 1) * 128], QR[hs][:, qb * 512:(qb + 1) * 512])],
                         [("KH", hs, kt // 4), ("QN", hs), ("QR", hs), "KRG"], [("ps", p)], nowait=True)
                return p

            def post_a(qb, kt, p, ptn):
                act(PT[ptn][:, :], PSB[p][:, :], AF.Exp, [("ps", p)], [("PT", ptn)], scale=SC_A)

            attn_sweep(64, score_a, post_a, lambda qb, kt, hs=hs: (VH2[hs][:, kt, :], [("VH", hs, kt // 4)]),
                       lambda qb, kt: (ONESB, "ONESB"),
                       lambda qb, po, pd, hd=hd: finalize(po, pd, GH[:, qb * 512:(qb + 1) * 512], ["GH"], hd, qb),
                       PT, T01, T23, T4, ACC=ACC, extras=nxt, every=16)

        if stop < 4:
            break
        phase()
        PT = [sbt([128, 512], BF16) for _ in range(8)]
        T01 = [sbt([128, 512], BF16) for _ in range(2)]
        T23 = [sbt([128, 512], BF16) for _ in range(2)]
        T4 = [sbt([128, 512], BF16) for _ in range(2)]
        RD = [sbt([128, 512], F32) for _ in range(2)]
        YT = [sbt([128, 512], F32) for _ in range(2)]
        KBX = [sbt([128, 4096], BF16) for _ in range(2)]
        VBX = [sbt([128, 32, 128], BF16) for _ in range(2)]
        QBh = [sbt([128, T], BF16) for _ in range(2)]
        GBh = [sbt([128, T], BF16) for _ in range(2)]
        TWh = [sbt([128, TWW], BF16) for _ in range(2)]

        KST = sbt([128, 2, 4, 1024], BF16)
        VST = sbt([128, 2, 4, 8, 128], BF16)

        def vtiles(src2d, tok0, ntok, hd):
            return src2d[tok0:tok0 + ntok, hd * 128:(hd + 1) * 128].rearrange("(n p) c -> p n c", p=128)

        def prep_b(hd):
            hs = hd % 2
            krow = KB_ROW + hd * 128
            kblk, koff = krow // 256, krow % 256
            dma("sp", KBX[hs][:, 1024:3072], XCH[krow:krow + 128, :], [("XCH", kblk)], [("KBX", hs, 1)], ("KBXo", hs))
            dma("sp", VBX[hs][:, 8:24, :], vtiles(VBv, 0, 2048, hd), [("XCH", b) for b in range(6, 10)], [("VBX", hs, 1)], ("VBXo", hs))
            dmas("sp", [(KST[:, 0, :, :], G3[kblk * 4:kblk * 4 + 4, koff:koff + 128, 1024:2048].rearrange("r f t -> f r t")),
                        (KST[:, 1, :, :], G3[kblk * 4:kblk * 4 + 4, koff:koff + 128, 0:1024].rearrange("r f t -> f r t"))],
                 [("G", kblk)], ["KST"], "KST")
            vp = []
            for r in range(4):
                for half in range(2):
                    for side, blk in ((0, 8 + half), (1, 6 + half)):
                        src = G3[blk * 4 + r, :, :].rearrange("f (two c) -> (f two) c", two=2)
                        vp.append((VST[:, side, r, half * 4:(half + 1) * 4, :], vtiles(src, 0, 512, hd)))
            dmas("sp", vp, [("G", b) for b in range(6, 10)], ["VST"], "VST")
            dma("sp", QBh[hs][:, :], QB[hd], [("qb", hd)], [("QBh", hs)], ("QBh", hs))
            dma("sp", GBh[hs][:, :], GB[hd], [("gb", hd)], [("GBh", hs)], ("GBh", hs))
            dma("sp", TWh[hs][:, :], TW[hd], [("TW", hd)], [("TWh", hs)], ("TWh", hs))
            sel = []
            for side, (dk, dv, nm) in enumerate(((KBX[hs][:, 0:1024], VBX[hs][:, 0:8, :], 0), (KBX[hs][:, 3072:4096], VBX[hs][:, 24:32, :], 2))):
                for r in range(4):
                    selc = FLG[:, 2 + side * 4 + r:3 + side * 4 + r]
                    if r == 0:
                        sel.append(lambda dk=dk, side=side, selc=selc, hs=hs, nm=nm: ts(
                            dk, KST[:, side, 0, :], selc, 0.0, ALU.mult, ALU.add, ["KST", "FLG"], [("KBX", hs, nm)]))
                        sel.append(lambda dv=dv, side=side, selc=selc, hs=hs, nm=nm: ts(
                            dv, VST[:, side, 0, :, :], selc, 0.0, ALU.mult, ALU.add, ["VST", "FLG"], [("VBX", hs, nm)]))
                    else:
                        sel.append(lambda dk=dk, side=side, r=r, selc=selc, hs=hs, nm=nm: stt(
                            dk, KST[:, side, r, :], selc, dk, ALU.mult, ALU.add, ["KST", "FLG", ("KBX", hs, nm)], [("KBX", hs, nm)]))
                        sel.append(lambda dv=dv, side=side, r=r, selc=selc, hs=hs, nm=nm: stt(
                            dv, VST[:, side, r, :, :], selc, dv, ALU.mult, ALU.add, ["VST", "FLG", ("VBX", hs, nm)], [("VBX", hs, nm)]))
            return sel

        for f in prep_b(0):
            f()
        for hd in range(NH):
            hs = hd % 2
            nxt = prep_b(hd + 1) if hd + 1 < NH else []

            def score_b(qb, i, hs=hs):
                kt = 4 * qb + i
                p = next_ps(5)
                mm_group(PSB[p][:, :], [(KBX[hs][:, kt * 128:(kt + 1) * 128], QBh[hs][:, qb * 512:(qb + 1) * 512]),
                                        (IDB[:, :], TWh[hs][:, (19 - i) * 128:(19 - i) * 128 + 512])],
                         [("KBX", hs, 0), ("KBX", hs, 1), ("KBX", hs, 2), ("QBh", hs), ("TWh", hs), "IDB"], [("ps", p)], nowait=True)
                return p

            def post_b(qb, i, p, ptn, hs=hs):
                act(PT[ptn][:, :], PSB[p][:, :], AF.Exp, [("ps", p)], [("PT", ptn)], scale=SC_B)

            def ones_b(qb, i):
                kt = 4 * qb + i
                if kt < 8:
                    return ONESL, "ONESL"
                if kt >= 24:
                    return ONESR, "ONESR"
                return ONESB, "ONESB"

            attn_sweep(20, score_b, post_b,
                       lambda qb, i, hs=hs: (VBX[hs][:, 4 * qb + i, :], [("VBX", hs, 0), ("VBX", hs, 1), ("VBX", hs, 2)]),
                       ones_b,
                       lambda qb, po, pd, hd=hd, hs=hs: finalize(po, pd, GBh[hs][:, qb * 512:(qb + 1) * 512], [("GBh", hs)], 8 + hd, qb,
                                                                 RD=RD, YT=YT),
                       PT, T01, T23, T4, extras=nxt)

        if stop < 5:
            break
        phase()
        WS3 = [sbt([128, 16, 256], BF16) for _ in range(3)]
        XO = [sbt([128, 512], F32) for _ in range(3)]
        XN = [sbt([128, 512], F32) for _ in range(3)]
        SQ3 = [sbt([128, 512], BF16) for _ in range(2)]
        TMP3 = sbt([128, 512], F32)
        RS3 = [sbt([128, 512], F32) for _ in range(2)]
        wov = w_out[l].rearrange("(c p) n -> p c n", p=128)
        n3 = 0
        for g in range(8):
            s = next_ws()
            dma("pool", WS3[s][:, :, :], wov[:, :, g * 256:(g + 1) * 256], [], [("WS", s)], ("WS", s))
            for ct in range(2):
                dc = g * 2 + ct
                for tb in range(4):
                    xs = n3 % 3
                    n3 += 1
                    dma("sp", XO[xs][:, :], xsv[:, dc, tb * 512:(tb + 1) * 512], [xt(dc, tb)], [("XO", xs)], ("XO", xs))
                    p = next_ps()
                    mm_group(PSB[p][:, :], [(WS3[s][:, c, ct * 128:(ct + 1) * 128], HY[:, c, tb * 512:(tb + 1) * 512]) for c in range(16)],
                             [("WS", s)] + hy_tb(tb), [("ps", p)])
                    stt(XN[xs][:, :], PSB[p][:, :], MOD[:, l, 32 + dc:33 + dc], XO[xs][:, :], ALU.mult, ALU.add,
                        [("ps", p), ("XO", xs), ("MOD", l)], [("XN", xs)])
                    dma("sp", xrv[:, dc, tb * 512:(tb + 1) * 512], XN[xs][:, :], [("XN", xs)], [xt(dc, tb)], ("XNst", xs))
                    q3 = n3 % 2
                    act(SQ3[q3][:, :], XN[xs][:, :], AF.Square, [("XN", xs)], [("SQ3", q3)])
                    mm1(PSB[4 + tb][:, :], ONESFb[:, :], SQ3[q3][:, :], dc == 0, dc == 15, [("SQ3", q3), "ONESFb"], [("ps", 4 + tb)])
        for tb in range(4):
            act(TMP3[:, :], PSB[4 + tb][:, :], AF.Sqrt, [("ps", 4 + tb), "EPSC"], ["TMP3"], scale=1.0, bias=EPSC[:, 0:1])
            recip(RS3[tb % 2][:, :], TMP3[:, :], ["TMP3"], [("RS3", tb % 2)])
            dma("sp", RSd[:, tb * 512:(tb + 1) * 512], RS3[tb % 2][:, :], [("RS3", tb % 2)], [("RSd", tb)], ("RS3st", tb % 2))

    phase()
    XL = [sbt([128, 4, 512], F32) for _ in range(2)]
    SQ = [sbt([128, 512], BF16) for _ in range(2)]
    TMP = sbt([128, 512], F32)
    RSTD = sbt([128, T], F32)
    XF = [sbt([128, 4, 512], F32) for _ in range(2)]
    xfv = (xres if (depth > 0 and stop >= 5) else xT).rearrange("(c p) t -> p c t", p=128)
    ov = outT.rearrange("(c p) t -> p c t", p=128)
    if depth > 0 and stop >= 5:
        li = 0
        for tb in range(4):
            dma("sp", RSTD[:, tb * 512:(tb + 1) * 512], RSd[:, tb * 512:(tb + 1) * 512], [("RSd", tb)], [("RSTD", tb)], ("RSld", tb))
    else:
        li = rstd_pass(xfv, XL, SQ, TMP, RSTD)
    outs = []
    for tb in range(4):
        for g in range(4):
            s = li % 2
            li += 1
            dma("sp", XL[s][:, :, :], xfv[:, 4 * g:4 * g + 4, tb * 512:(tb + 1) * 512],
                [xt(4 * g + k, tb) for k in range(4)], [("XL", s)], ("XL", s))
            for k in range(4):
                c = 4 * g + k
                stt(XF[s][:, k, :], XL[s][:, k, :], FNG[:, c:c + 1], RSTD[:, tb * 512:(tb + 1) * 512], ALU.mult, ALU.mult,
                    [("XL", s), ("RSTD", tb), "FNG"], [("XF", s, k)])
            outs.append(dma("sp", ov[:, 4 * g:4 * g + 4, tb * 512:(tb + 1) * 512], XF[s][:, :, :], [("XF", s, k) for k in range(4)],
                            [("out", g, tb)], ("XFst", s)))
    SC.final = outs
    SC.emit()
    return nc


def _t5_buckets(rel):
    nb = 16
    max_exact = nb // 2
    base = np.where(rel > 0, nb, 0)
    n = np.abs(rel)
    large = max_exact + (np.log(np.maximum(n, 1) / max_exact) / math.log(1024 / max_exact) * (nb - max_exact)).astype(np.int32)
    large = np.minimum(large, nb - 1)
    return (base + np.where(n < max_exact, n, large)).astype(np.int32)


_PROG = {}


def kernel(x, c, positions, norm_g, ada_w, ada_b, w_in, q_a_norm_g, w_q_up, kv_a_norm_g, w_kv_up, rel_bias, w_out,
           final_norm_g, _depth=DEPTH, _stop=99):
    f32 = np.float32
    x = np.asarray(x, f32)
    c = np.asarray(c, f32)
    positions = np.asarray(positions, np.int32)
    kk = np.arange(128)[:, None]
    nn = np.arange(TWW)[None, :]
    off = kk - nn + 1408
    mult = ((np.abs(off) <= 64).astype(np.int32) + ((np.abs(off) <= 256) & (off % 4 == 0)).astype(np.int32)
            + ((np.abs(off) <= 1024) & (off % 16 == 0)).astype(np.int32))
    logm = np.where(mult > 0, np.log(np.maximum(mult, 1)), -30000.0).astype(f32)
    bidx = _t5_buckets(np.clip(off, -1024, 1024))
    rb = np.asarray(rel_bias, f32)
    biasTw = np.ascontiguousarray(np.transpose(rb[bidx], (2, 0, 1)))
    inv = (1.0 / (10000.0 ** (np.arange(0, 64, 2, dtype=f32) / 64.0))).astype(f32)
    rconst = np.zeros((64, 2), f32)
    rconst[:, 0] = np.concatenate([inv, inv])
    rconst[:, 1] = np.concatenate([-np.ones(32, f32), np.ones(32, f32)])

    shared = {
        "normg": np.ascontiguousarray(np.asarray(norm_g, f32).reshape(DEPTH, 16, 128).transpose(2, 0, 1)),
        "adab": np.ascontiguousarray(np.asarray(ada_b, f32).reshape(DEPTH, 48, 128).transpose(2, 0, 1)),
        "qg": np.ascontiguousarray(np.asarray(q_a_norm_g, f32).reshape(DEPTH, 4, 128).transpose(2, 0, 1)),
        "kvg": np.ascontiguousarray(np.asarray(kv_a_norm_g, f32).reshape(DEPTH, 2, 128).transpose(2, 0, 1)),
        "fng": np.ascontiguousarray(np.asarray(final_norm_g, f32).reshape(16, 128).T),
        "rconst": rconst, "biasTw": biasTw, "logm": logm, "ident": np.eye(128, dtype=f32),
        "w_in": np.asarray(w_in, f32), "w_q_up": np.asarray(w_q_up, f32),
        "w_kv_up": np.asarray(w_kv_up, f32), "w_out": np.asarray(w_out, f32),
    }
    in_maps = []
    for core in range(8):
        b, j = core // 4, core % 4
        t0 = j * T
        m = dict(shared)
        m["xT"] = np.ascontiguousarray(x[b, t0:t0 + T, :].T)
        m["cT"] = np.ascontiguousarray(c[b].reshape(16, 128).T)
        m["pos"] = np.ascontiguousarray(positions[b, t0:t0 + T][None, :])
        fl = np.zeros((128, 10), f32)
        fl[:, 0] = 1.0 if j > 0 else 0.0
        fl[:, 1] = 1.0 if j < 3 else 0.0
        if j > 0:
            fl[:, 2 + (j - 1)] = 1.0
        if j < 3:
            fl[:, 6 + (j + 1)] = 1.0
        m["flags"] = fl
        m["ada_wp"] = np.ascontiguousarray(np.asarray(ada_w, f32)[:, :, j * 1536:(j + 1) * 1536])
        in_maps.append(m)
    if (_depth, _stop) not in _PROG:
        _PROG[(_depth, _stop)] = build_program(_depth, _stop)
    res = run_bass_kernel_spmd(_PROG[(_depth, _stop)], in_maps, core_ids=list(range(8)))
    out = np.empty((2, S, D), f32)
    for core in range(8):
        b, j = core // 4, core % 4
        out[b, j * T:(j + 1) * T, :] = np.asarray(res.results[core]["outT"]).T
    return out
```
